# Optimizing a Trainium2 kernel written in Bass

```python
import jax, jax.numpy as jnp
from jax import lax
import numpy as np

D_MODEL = 4096
BATCH = 2
SEQ = 8192
DEPTH = 1

CHUNK = 64
N_META = 16
N_PAD = CHUNK - N_META
EPS = 1e-6
NEG = -1e30

ML_HEADS = 8
ML_QK_DIM = D_MODEL // 16
ML_V_DIM = D_MODEL // 8
ML_QK_WIDTH = ML_HEADS * ML_QK_DIM
ML_WIDTH = ML_HEADS * ML_V_DIM
GATE_CAP = 15.0

GDN_HEADS = 32
GDN_HEAD_DIM = D_MODEL // 32
GDN_WIDTH = GDN_HEADS * GDN_HEAD_DIM
CONV_WIDTH = 4

IN_SPLITS = (ML_QK_WIDTH, ML_QK_WIDTH, ML_WIDTH, ML_WIDTH, ML_WIDTH, ML_HEADS, ML_HEADS,
             3 * GDN_WIDTH, GDN_WIDTH, GDN_HEADS, GDN_HEADS, D_MODEL, D_MODEL)
D_IN = sum(IN_SPLITS)

kernel_name = "hybrid_mlstm_gdn_gated_merge"


def rmsnorm(x, g):
    xf = x.astype(jnp.float32)
    y = xf * lax.rsqrt(jnp.mean(xf * xf, -1, keepdims=True) + EPS)
    return (y * g.astype(jnp.float32)).astype(x.dtype)


def l2norm(x):
    xf = x.astype(jnp.float32)
    return xf * lax.rsqrt(jnp.sum(xf * xf, -1, keepdims=True) + EPS)


def split_heads(t, n_heads):
    B, S = t.shape[:2]
    return t.reshape(B, S, n_heads, -1)


def chunk_heads(t):
    B, S, H = t.shape[:3]
    t = t.reshape(B, S // CHUNK, CHUNK, H, *t.shape[3:])
    perm = (1, 0, 3, 2) + tuple(range(4, t.ndim))
    return t.transpose(perm)


def unchunk_heads(t):
    N, B, H, L, d = t.shape
    return t.transpose(1, 0, 3, 2, 4).reshape(B, N * L, H, d)


def causal_depthwise_conv(u, w):
    K, C = w.shape
    return lax.conv_general_dilated(u, w[:, None, :].astype(u.dtype), window_strides=(1,),
                                    padding=[(K - 1, 0)], dimension_numbers=('NWC', 'WIO', 'NWC'),
                                    feature_group_count=C)


def mlstm_chunkwise(q, k, v, i_pre, logf):
    B, S, H, dk = q.shape
    dv = v.shape[-1]
    f32 = jnp.float32
    qc = chunk_heads(q.astype(f32))
    kc = chunk_heads(k.astype(f32)) * (dk ** -0.5)
    vc = chunk_heads(v.astype(f32))
    ic = chunk_heads(i_pre)
    bcum = jnp.cumsum(chunk_heads(logf), axis=-1)
    causal = jnp.tril(jnp.ones((CHUNK, CHUNK), bool))
    dmat = jnp.where(causal, bcum[..., :, None] - bcum[..., None, :] + ic[..., None, :], NEG)
    dmax = jnp.max(dmat, -1)
    s = jnp.einsum('nbhid,nbhjd->nbhij', qc, kc) * jnp.exp(dmat - dmax[..., None])
    num_intra = jnp.einsum('nbhij,nbhjd->nbhid', s, vc)
    den_intra = jnp.sum(s, -1)
    g_tot = bcum[..., -1]
    a_log = g_tot[..., None] - bcum + ic

    def step(carry, xs):
        C, n, m = carry
        q_, k_, v_, b_, a_, g_, dmax_, num_, den_ = xs
        inter = b_ + m[..., None]
        m_i = jnp.maximum(dmax_, inter)
        w_inter = jnp.exp(inter - m_i)
        w_intra = jnp.exp(dmax_ - m_i)
        num = w_inter[..., None] * jnp.einsum('bhid,bhde->bhie', q_, C) + w_intra[..., None] * num_
        den = w_inter * jnp.einsum('bhid,bhd->bhi', q_, n) + w_intra * den_
        h = num / jnp.maximum(jnp.abs(den), jnp.exp(-m_i))[..., None]
        m_new = jnp.maximum(g_ + m, jnp.max(a_, -1))
        decay = jnp.exp(g_ + m - m_new)
        wk = jnp.exp(a_ - m_new[..., None])[..., None] * k_
        C = decay[..., None, None] * C + jnp.einsum('bhld,bhle->bhde', wk, v_)
        n = decay[..., None] * n + jnp.sum(wk, -2)
        return (C, n, m_new), h

    init = (jnp.zeros((B, H, dk, dv), f32), jnp.zeros((B, H, dk), f32), jnp.zeros((B, H), f32))
    _, h = lax.scan(step, init, (qc, kc, vc, bcum, a_log, g_tot, dmax, num_intra, den_intra))
    return unchunk_heads(h)


def gated_delta_chunkwise(q, k, v, g, beta):
    B, S, H, dk = q.shape
    dv = v.shape[-1]
    f32 = jnp.float32
    qc = chunk_heads(q) * (dk ** -0.5)
    kc = chunk_heads(k)
    vc = chunk_heads(v.astype(f32))
    gc = jnp.cumsum(chunk_heads(g), -1)
    bc = chunk_heads(beta)
    incl = jnp.tril(jnp.ones((CHUNK, CHUNK), bool))
    strict = jnp.tril(jnp.ones((CHUNK, CHUNK), bool), -1)
    decay_incl = jnp.exp(jnp.where(incl, gc[..., :, None] - gc[..., None, :], NEG))
    kk = jnp.einsum('nbhid,nbhjd->nbhij', kc, kc)
    a_strict = jnp.where(strict, kk * decay_incl, 0.0) * bc[..., :, None]
    eye = jnp.eye(CHUNK, dtype=f32)
    rhs = jnp.concatenate([vc * bc[..., None], kc * (bc * jnp.exp(gc))[..., None]], -1)
    uw = lax.linalg.triangular_solve(eye + a_strict, rhs, left_side=True, lower=True,
                                     unit_diagonal=True)
    u, w = uw[..., :dv], uw[..., dv:]
    qk = jnp.einsum('nbhid,nbhjd->nbhij', qc, kc) * decay_incl
    q_dec = qc * jnp.exp(gc)[..., None]
    g_last = gc[..., -1]
    k_dec = kc * jnp.exp(g_last[..., None] - gc)[..., None]

    def step(S_, xs):
        qk_, qd_, kd_, u_, w_, gl_ = xs
        v_new = u_ - jnp.einsum('bhld,bhde->bhle', w_, S_)
        o = jnp.einsum('bhld,bhde->bhle', qd_, S_) + jnp.einsum('bhij,bhje->bhie', qk_, v_new)
        S_ = jnp.exp(gl_)[..., None, None] * S_ + jnp.einsum('bhld,bhle->bhde', kd_, v_new)
        return S_, o

    _, o = lax.scan(step, jnp.zeros((B, H, dk, dv), f32), (qk, q_dec, k_dec, u, w, g_last))
    return unchunk_heads(o)


def hybrid_layer(x, valid, norm_g, w_in, b_igate, b_fgate, ml_norm_g, conv_w, a_log, dt_bias,
                 gdn_norm_g, w_proj_a, w_proj_b, w_out):
    B, S, _ = x.shape
    f32 = jnp.float32
    vmask = valid[None, :, None]
    h = rmsnorm(x, norm_g)
    proj = h @ w_in.astype(h.dtype)
    (m_q, m_k, m_v, m_o, m_z, m_i, m_f, g_qkv, g_z, g_a, g_b, gate_a, gate_b) = jnp.split(
        proj, np.cumsum(IN_SPLITS)[:-1].tolist(), axis=-1)

    i_pre = GATE_CAP * jnp.tanh((m_i.astype(f32) + b_igate) / GATE_CAP)
    f_pre = GATE_CAP * jnp.tanh((m_f.astype(f32) + b_fgate) / GATE_CAP)
    i_pre = jnp.where(vmask, i_pre, NEG)
    logf = jnp.where(vmask, jax.nn.log_sigmoid(f_pre), 0.0)
    hm = mlstm_chunkwise(split_heads(m_q, ML_HEADS), split_heads(m_k, ML_HEADS),
                         split_heads(m_v, ML_HEADS), i_pre, logf)
    hm = rmsnorm(hm, ml_norm_g.reshape(ML_HEADS, ML_V_DIM)).reshape(B, S, ML_WIDTH)
    y_a = (hm * jax.nn.sigmoid(m_o.astype(f32)) * jax.nn.silu(m_z.astype(f32))).astype(x.dtype)

    qkv = jax.nn.silu(causal_depthwise_conv(g_qkv, conv_w))
    c_q, c_k, c_v = jnp.split(qkv, 3, axis=-1)
    c_q = l2norm(split_heads(c_q, GDN_HEADS))
    c_k = l2norm(split_heads(c_k, GDN_HEADS))
    c_v = split_heads(c_v, GDN_HEADS)
    beta = jnp.where(vmask, jax.nn.sigmoid(g_b.astype(f32)), 0.0)
    g = jnp.where(vmask, -jnp.exp(a_log.astype(f32)) * jax.nn.softplus(g_a.astype(f32) + dt_bias), 0.0)
    og = gated_delta_chunkwise(c_q, c_k, c_v, g, beta)
    og = rmsnorm(og, gdn_norm_g) * jax.nn.silu(split_heads(g_z, GDN_HEADS).astype(f32))
    y_b = og.reshape(B, S, GDN_WIDTH).astype(x.dtype)

    merged = (jax.nn.sigmoid(gate_a) * (y_a @ w_proj_a.astype(x.dtype))
              + jax.nn.sigmoid(gate_b) * (y_b @ w_proj_b.astype(x.dtype)))
    return x + merged @ w_out.astype(x.dtype)


def setup_inputs(seed: int = 0) -> dict:
    key = jax.random.key(seed)
    ks = jax.random.split(key, 16)
    f32 = jnp.float32
    nrm = lambda k, s: jax.random.normal(k, s, f32)
    x = nrm(ks[0], (BATCH, SEQ, D_MODEL))
    meta = nrm(ks[1], (N_META, D_MODEL))
    norm_in_g = 1.0 + 0.02 * nrm(ks[2], (DEPTH, D_MODEL))
    w_in = nrm(ks[3], (DEPTH, D_MODEL, D_IN)) * (D_MODEL ** -0.5)
    b_igate = 0.1 * nrm(ks[4], (DEPTH, ML_HEADS))
    b_fgate = 3.0 + 0.5 * nrm(ks[5], (DEPTH, ML_HEADS))
    ml_norm_g = 1.0 + 0.02 * nrm(ks[6], (DEPTH, ML_WIDTH))
    conv_w = nrm(ks[7], (DEPTH, CONV_WIDTH, 3 * GDN_WIDTH)) * (CONV_WIDTH ** -0.5)
    a_log = jnp.log(jax.random.uniform(ks[8], (DEPTH, GDN_HEADS), f32, 1.0, 16.0))
    dt = jnp.exp(jax.random.uniform(ks[9], (DEPTH, GDN_HEADS), f32, np.log(1e-3), np.log(1e-1)))
    dt_bias = dt + jnp.log(-jnp.expm1(-dt))
    gdn_norm_g = 1.0 + 0.02 * nrm(ks[10], (DEPTH, GDN_HEAD_DIM))
    w_proj_a = nrm(ks[11], (DEPTH, ML_WIDTH, D_MODEL)) * (ML_WIDTH ** -0.5)
    w_proj_b = nrm(ks[12], (DEPTH, GDN_WIDTH, D_MODEL)) * (GDN_WIDTH ** -0.5)
    w_out = nrm(ks[13], (DEPTH, D_MODEL, D_MODEL)) * (D_MODEL ** -0.5)
    norm_f_g = 1.0 + 0.02 * nrm(ks[14], (D_MODEL,))
    return {"x": x, "meta": meta, "norm_in_g": norm_in_g, "w_in": w_in, "b_igate": b_igate,
            "b_fgate": b_fgate, "ml_norm_g": ml_norm_g, "conv_w": conv_w, "a_log": a_log,
            "dt_bias": dt_bias, "gdn_norm_g": gdn_norm_g, "w_proj_a": w_proj_a,
            "w_proj_b": w_proj_b, "w_out": w_out, "norm_f_g": norm_f_g}


def reference(x, meta, norm_in_g, w_in, b_igate, b_fgate, ml_norm_g, conv_w, a_log, dt_bias,
              gdn_norm_g, w_proj_a, w_proj_b, w_out, norm_f_g):
    B = x.shape[0]
    dtype = x.dtype
    h = jnp.concatenate([jnp.zeros((B, N_PAD, D_MODEL), dtype),
                         jnp.broadcast_to(meta.astype(dtype), (B, N_META, D_MODEL)), x], axis=1)
    valid = jnp.arange(h.shape[1]) >= N_PAD
    for l in range(DEPTH):
        h = hybrid_layer(h, valid, norm_in_g[l], w_in[l], b_igate[l], b_fgate[l], ml_norm_g[l],
                         conv_w[l], a_log[l], dt_bias[l], gdn_norm_g[l], w_proj_a[l], w_proj_b[l],
                         w_out[l])
    h = rmsnorm(h, norm_f_g)
    return h[:, N_PAD + N_META:]
```

```python
import numpy as np
from contextlib import ExitStack
import concourse.bass as bass
import concourse.mybir as mybir
from concourse.bass_utils import run_bass_kernel_spmd

F32 = mybir.dt.float32
BF16 = mybir.dt.bfloat16
AF = mybir.ActivationFunctionType
ALU = mybir.AluOpType
AX = mybir.AxisListType

D = 4096
EPS = 1e-6
NEG = -30000.0
import os
SKIP_GDN = bool(int(os.environ.get('SKIP_GDN', '0')))
SKIP_ML = bool(int(os.environ.get('SKIP_ML', '0')))
GS = int(os.environ.get('GDN_STOP', '99'))
GF_POOL = bool(int(os.environ.get('GF_POOL', '0')))
INTERLEAVE = bool(int(os.environ.get('INTERLEAVE', '1')))
P3S = int(os.environ.get('P3_STOP', '99'))
GM = int(os.environ.get('GM', '5'))
GI = int(os.environ.get('GI', '9'))
LE = os.environ.get('LE', 'act')
LX = ['ZPz'] if os.environ.get('LX') else []
W1C = 8212
C_MQ, C_MK, C_MV, C_MO, C_MZ, C_GQ, C_GK, C_GV, C_GZ, C_SM = 0, 512, 1024, 2048, 3072, 4096, 5120, 6144, 7168, 8192


PSUM_KEYS = ('PA', 'PB', 'PT', 'pg', 'ptr', 'psel', 'pq')


class Sched:
    CAP = 12000
    NDMA = 8

    def __init__(self, nc, same_engine_sync=True):
        self.nc = nc
        self.same = same_engine_sync
        self.compute = ('pe', 'act', 'dve', 'pool', 'sp')
        self.ops = {e: [] for e in self.compute}
        self.seq = {}
        self.lastw = {}
        self.readers = {}
        self.dma_rr = {e: 0 for e in self.compute}

    @staticmethod
    def _unit(s):
        return 16 if isinstance(s, tuple) else 1

    def op(self, eng, fn, reads=(), writes=(), dma=False, signal=False):
        if dma:
            k = self.dma_rr[eng] % self.NDMA
            self.dma_rr[eng] += 1
            stream = (eng, k)
        else:
            stream = eng
        deps = set()
        xr = [b for b in reads if isinstance(b, tuple) and b[0] in PSUM_KEYS]
        if xr:
            writes = list(writes) + [b for b in xr if b not in writes]
        for b in reads:
            t = self.lastw.get(b)
            if t is not None:
                deps.add(t)
        for b in writes:
            t = self.lastw.get(b)
            if t is not None:
                deps.add(t)
            for t in self.readers.get(b, ()):
                deps.add(t)
        fdeps = []
        for (s, i) in deps:
            if s == eng and (eng == 'pe' or not self.same):
                continue
            fdeps.append((s, i))
        self.seq[stream] = self.seq.get(stream, 0) + 1
        tok = (stream, self.seq[stream])
        self.ops[eng].append(dict(fn=fn, deps=fdeps, tok=tok, dma=dma or signal))
        for b in writes:
            self.lastw[b] = tok
            self.readers[b] = []
        for b in reads:
            self.readers.setdefault(b, []).append(tok)
        return tok

    def barrier(self):
        toks = [(s, n) for s, n in self.seq.items()]
        for e in self.compute:
            self.ops[e].append(dict(fn=None, deps=[t for t in toks if t[0] != e], tok=None, dma=False))

    def emit(self, semstack, blockstack):
        nc = self.nc
        needed = {}
        for e in self.compute:
            for o in self.ops[e]:
                for (s, i) in o['deps']:
                    needed.setdefault(s, set()).add(i)
                if o['dma'] and o['tok'] is not None:
                    needed.setdefault(o['tok'][0], set()).add(o['tok'][1])
        rank, sems = {}, {}
        for s, st in needed.items():
            srt = sorted(st)
            rank[s] = {v: r + 1 for r, v in enumerate(srt)}
            n = (len(srt) + self.CAP - 1) // self.CAP
            nm = s if isinstance(s, str) else f"{s[0]}d{s[1]}"
            sems[s] = [semstack.enter_context(nc.semaphore(_un(f"s_{nm}_{j}"))) for j in range(n)]

        def tokval(s, i):
            r = rank[s][i]
            return (r - 1) // self.CAP, ((r - 1) % self.CAP + 1) * self._unit(s)

        block = blockstack.enter_context(nc.Block())
        engobj = {'pe': 'tensor', 'act': 'scalar', 'dve': 'vector', 'pool': 'gpsimd', 'sp': 'sync'}
        for e in self.compute:
            oplist = self.ops[e]
            if not oplist:
                continue

            def body(eng, oplist=oplist):
                seen = {}
                for o in oplist:
                    for (s, i) in o['deps']:
                        semidx, val = tokval(s, i)
                        cur = seen.get(s, (-1, 0))
                        if cur[0] > semidx or (cur[0] == semidx and cur[1] >= val):
                            continue
                        seen[s] = (semidx, val)
                        eng.wait_ge(sems[s][semidx], val)
                    if o['fn'] is None:
                        continue
                    ins = o['fn'](eng)
                    s, i = o['tok']
                    if s in rank and i in rank[s]:
                        semidx, _ = tokval(s, i)
                        ins.then_inc(sems[s][semidx], self._unit(s))

            getattr(block, engobj[e])(body)


def f_dma(out, in_):
    return lambda e: e.dma_start(out=out, in_=in_)


def f_mm(out, lhsT, rhs, start=True, stop=True):
    return lambda e: e.matmul(out, lhsT=lhsT, rhs=rhs, start=start, stop=stop)


def f_tr(out, in_, ident):
    return lambda e: e.transpose(out=out, in_=in_, identity=ident)


def f_act(out, in_, func, bias=0.0, scale=1.0, accum_out=None):
    if accum_out is None:
        return lambda e: e.activation(out=out, in_=in_, func=func, bias=bias, scale=scale)
    return lambda e: e.activation(out=out, in_=in_, func=func, bias=bias, scale=scale, accum_out=accum_out)


def f_copy(out, in_):
    return lambda e: e.tensor_copy(out=out, in_=in_)


def f_acopy(out, in_):
    return lambda e: e.copy(out=out, in_=in_)


def f_tt(out, in0, in1, op):
    return lambda e: e.tensor_tensor(out=out, in0=in0, in1=in1, op=op)


def f_ts(out, in0, s1, s2, op0, op1=None):
    if op1 is None:
        return lambda e: e.tensor_scalar(out=out, in0=in0, scalar1=s1, scalar2=None, op0=op0)
    return lambda e: e.tensor_scalar(out=out, in0=in0, scalar1=s1, scalar2=s2, op0=op0, op1=op1)


def f_stt(out, in0, scalar, in1, op0, op1):
    return lambda e: e.scalar_tensor_tensor(out=out, in0=in0, scalar=scalar, in1=in1, op0=op0, op1=op1)


def f_red(out, in_, op=None, axis=None):
    return lambda e: e.tensor_reduce(out=out, in_=in_, axis=axis or AX.X, op=op or ALU.add)


def f_memset(ap, v):
    return lambda e: e.memset(ap, v)


def f_recip(out, in_):
    return lambda e: e.reciprocal(out=out, in_=in_)


def f_asel(out, in_, pattern, op, fill, base, cm):
    return lambda e: e.affine_select(out=out, in_=in_, pattern=pattern, compare_op=op, fill=fill, base=base,
                                     channel_multiplier=cm)


_UN = [0]


def _un(n):
    _UN[0] += 1
    return f"{n}_{_UN[0]}"


class PSplit:
    def __init__(self, a, b):
        self.a, self.b = a, b

    def __getitem__(self, key):
        rows, cols = key
        c0, c1 = cols.start, cols.stop
        if c1 <= 4096:
            return self.a[rows, c0:c1]
        assert c0 >= 4096
        return self.b[rows, c0 - 4096:c1 - 4096]


class K:
    pass


def make_consts(k, S, st):
    nc = k.nc
    sb = lambda n, sh, dt=F32: st.enter_context(nc.sbuf_tensor(_un(n), sh, dt))
    k.identf = sb("identf", [128, 128])
    k.identb = sb("identb", [128, 128], BF16)
    S.op('pool', f_memset(k.identf[:], 1.0), writes=['identf'])
    S.op('pool', f_asel(k.identf[:], k.identf[:], [[-1, 128]], ALU.is_equal, 0.0, 0, 1), reads=['identf'], writes=['identf'])
    S.op('dve', f_copy(k.identb[:], k.identf[:]), reads=['identf'], writes=['identb'])


def emit_exchange(k, S, c, reads):
    CR = 256
    groups = getattr(k, 'groups', [[0, 1, 2, 3], [4, 5, 6, 7]])
    S.op('pool', lambda e: e.collective_compute("AllGather", ALU.bypass, replica_groups=groups,
                                                ins=[k.Y[c * CR:(c + 1) * CR, :]],
                                                outs=[k.YG[c * 4 * CR:(c + 1) * 4 * CR, :]]),
         reads=reads, writes=[('YG', c)], signal=True)


def emit_casts(k, S, which, defer=False):
    later = []
    def cast_w(src, dst, ncols, key, gw=512):
        ncg = (ncols + gw - 1) // gw
        for cg in range(ncg):
            c0 = cg * gw
            cw = min(gw, ncols - c0)
            for q in range(4):
                in_ = src[q * 1024:(q + 1) * 1024, c0:c0 + cw].rearrange("(kc p) n -> p kc n", p=128)
                out = dst[cg, :, q * 8:(q + 1) * 8, 0:cw]
                if defer:
                    later.append((out, in_, (key, cg, q)))
                else:
                    S.op('pool', f_dma(out, in_), writes=[(key, cg, q)], dma=True)
    if 'w1' in which:
        cast_w(k.w1, k.w1b, W1C, 'w1b')
    if 'wg' in which:
        cast_w(k.wg, k.wgb, 8192, 'wgb', 128)
    if 'wa' in which:
        cast_w(k.wa, k.wab, 4096, 'wab', 128)
    if 'wb' in which:
        cast_w(k.wb, k.wbb, 4096, 'wbb', 128)
    if 'wo' in which:
        cast_w(k.wo, k.wob, 4096, 'wob')
    return later


def phase1(k, semstack):
    nc = k.nc
    NT = k.NT
    TB = 512
    S = Sched(nc)
    with ExitStack() as st:
        sb = lambda n, sh, dt=F32: st.enter_context(nc.sbuf_tensor(_un(n), sh, dt))
        ps = lambda n, sh, dt=F32: st.enter_context(nc.psum_tensor(_un(n), sh, dt))
        make_consts(k, S, st)
        emit_casts(k, S, ('w1',))
        later = emit_casts(k, S, ('wg', 'wa', 'wb', 'wo'), defer=True)
        zt = sb("zt", [64, 1024])
        S.op('dve', f_memset(zt[:], 0.0), writes=['zt'])
        for c0_ in range(0, W1C, 1024):
            cw_ = min(1024, W1C - c0_)
            S.op('sp', f_dma(k.P[0:64, c0_:c0_ + cw_], zt[:, 0:cw_]), reads=['zt'], writes=[('Pmargin', c0_)], dma=True)

        gbc = sb("gbc", [128, D])
        S.op('sp', f_dma(gbc[:], k.norm_in_g.partition_broadcast(128)), writes=['gbc'], dma=True)
        xt = [sb(f"xt{i}", [128, D]) for i in range(2)]
        hn = [sb(f"hn{i}", [128, D], BF16) for i in range(2)]
        ss = [sb(f"ss{i}", [128, 1]) for i in range(2)]
        rs = [sb(f"rs{i}", [128, 1]) for i in range(2)]
        hT = sb("hT", [128, 32, TB], BF16)
        wt = [sb(f"wt{i}", [128, 32, 512], BF16) for i in range(3)]
        ost = [sb(f"ost{i}", [128, 512]) for i in range(4)]
        ptr = [ps(f"ptr{i}", [128, 8, 128], BF16) for i in range(2)]
        pg = [ps(f"pg{i}", [128, 512]) for i in range(4)]

        blocks = [(0, 64)]
        t = 64
        while t < NT:
            n = min(TB, NT - t)
            blocks.append((t, n))
            t += n
        xi = 0
        tri = 0
        gi = 0
        wi = 0
        ncg = (W1C + 511) // 512
        for (t0, nt) in blocks:
            tiles = [(a, min(128, nt - a)) for a in range(0, nt, 128)]
            for (a, np_) in tiles:
                i = xi % 2
                xi += 1
                tok = t0 + a
                if tok == 0:
                    S.op('dve', f_memset(xt[i][0:64, :], 0.0), writes=[('xt', i)])
                    S.op('sp', f_dma(xt[i][48:64, :], k.meta), reads=[('xt', i)], writes=[('xt', i)], dma=True)
                else:
                    S.op('sp', f_dma(xt[i][0:np_, :], k.x[tok - 64:tok - 64 + np_, :]), writes=[('xt', i)], dma=True)
                S.op('act', f_act(hn[i][0:np_, :], xt[i][0:np_, :], AF.Square, accum_out=ss[i][0:np_, :]),
                     reads=[('xt', i)], writes=[('hn', i), ('ss', i)])
                S.op('act', f_act(rs[i][0:np_, :], ss[i][0:np_, :], AF.Sqrt, bias=EPS, scale=1.0 / D),
                     reads=[('ss', i)], writes=[('rs', i)])
                S.op('dve', f_recip(rs[i][0:np_, :], rs[i][0:np_, :]), reads=[('rs', i)], writes=[('rs', i)])
                S.op('dve', f_stt(hn[i][0:np_, :], xt[i][0:np_, :], rs[i][0:np_, 0:1], gbc[0:np_, :], ALU.mult, ALU.mult),
                     reads=[('xt', i), ('rs', i), 'gbc'], writes=[('hn', i)])
                for kc0 in range(0, 32, 8):
                    j = tri % 2
                    tri += 1
                    for q in range(8):
                        kc = kc0 + q
                        S.op('pe', f_tr(ptr[j][:, q, 0:np_], hn[i][0:np_, kc * 128:(kc + 1) * 128], k.identb[0:np_, 0:np_]),
                             reads=[('hn', i), 'identb'], writes=[('ptr', j)])
                    eng = 'act' if (tri % 2) else 'dve'
                    fn = f_acopy if eng == 'act' else f_copy
                    S.op(eng, fn(hT[:, kc0:kc0 + 8, a:a + np_], ptr[j][:, :, 0:np_]), reads=[('ptr', j)], writes=['hT'])
            for cg in range(ncg):
                c0 = cg * 512
                cw = min(512, W1C - c0)
                w = wi % 3
                wi += 1
                S.op('sp', f_dma(wt[w][:, :, 0:cw], k.w1b[cg, :, :, 0:cw]), reads=[('w1b', cg, q) for q in range(4)],
                     writes=[('wt', w)], dma=True)
                for (a, np_) in tiles:
                    g = gi % 4
                    gi += 1
                    for kc in range(32):
                        S.op('pe', f_mm(pg[g][0:np_, 0:cw], hT[:, kc, a:a + np_], wt[w][:, kc, 0:cw], kc == 0, kc == 31),
                             reads=['hT', ('wt', w)], writes=[('pg', g)])
                    eng = 'act' if (gi % 2) else 'dve'
                    fn = f_acopy if eng == 'act' else f_copy
                    S.op(eng, fn(ost[g][0:np_, 0:cw], pg[g][0:np_, 0:cw]), reads=[('pg', g)], writes=[('ost', g)])
                    r0 = 64 + t0 + a
                    S.op('pool', f_dma(k.P[r0:r0 + np_, c0:c0 + cw], ost[g][0:np_, 0:cw]), reads=[('ost', g)],
                         writes=[('P', gi)], dma=True)
            bi_ = blocks.index((t0, nt))
            lo_, hi_ = bi_ * len(later) // len(blocks), (bi_ + 1) * len(later) // len(blocks)
            for (o_, i_, key_) in later[lo_:hi_]:
                S.op('pool', f_dma(o_, i_), writes=[key_], dma=True)
        S.barrier()
        S.emit(semstack, st)


def build(NCH, stop_after=None):
    k = K()
    nc = bass.Bass("TRN2", target_bir_lowering=False)
    k.nc = nc
    k.NCH = NCH
    k.NT = 64 * (NCH + 1)
    SR = 64 * NCH
    k.TQ = SR // 4
    dt_in = lambda n, sh: nc.dram_tensor(n, sh, F32, kind="ExternalInput").ap()
    k.x = dt_in("x", [SR, D])
    k.meta = dt_in("meta", [16, D])
    k.norm_in_g = dt_in("norm_in_g", [D])
    k.w1 = dt_in("w1", [D, W1C])
    k.wg = dt_in("wg", [D, 8192])
    k.wa = dt_in("wa", [D, D])
    k.wb = dt_in("wb", [D, D])
    k.wo = dt_in("wo", [D, D])
    k.b_ig = dt_in("b_ig", [2])
    k.b_fg = dt_in("b_fg", [2])
    k.ml_g = dt_in("ml_g", [1024])
    k.conv_w = dt_in("conv_w", [4, 3072])
    k.a_log = dt_in("a_log", [8])
    k.dt_bias = dt_in("dt_bias", [8])
    k.gdn_g = dt_in("gdn_g", [128])
    k.norm_f_g = dt_in("norm_f_g", [D])
    k.xq = dt_in("xq", [k.TQ, D])
    k.yrows = nc.dram_tensor("yrows", [128, (k.TQ // 128) * 4], mybir.dt.int32, kind="ExternalInput").ap()
    k.out = nc.dram_tensor("out", [k.TQ, D], F32, kind="ExternalOutput").ap()
    if stop_after == 1:
        k.dbgP = nc.dram_tensor("dbgP", [64 + k.NT, W1C], F32, kind="ExternalOutput").ap()
    k.P = PSplit(nc.dram_tensor("P1", [64 + k.NT + 64, 4096], F32).ap(),
                 nc.dram_tensor("P2", [64 + k.NT + 64, W1C - 4096], F32).ap())
    k.w1b = nc.dram_tensor("w1b", [17, 128, 32, 512], BF16).ap()
    k.wgb = nc.dram_tensor("wgb", [64, 128, 32, 128], BF16).ap()
    k.wab = nc.dram_tensor("wab", [32, 128, 32, 128], BF16).ap()
    k.wbb = nc.dram_tensor("wbb", [32, 128, 32, 128], BF16).ap()
    k.wob = nc.dram_tensor("wob", [8, 128, 32, 512], BF16).ap()
    k.Y = nc.dram_tensor("Y", [SR, 2048], BF16).ap()
    k.YG = nc.dram_tensor("YG", [4 * SR, 2048], BF16).ap()
    k.TT = min(512, k.TQ)
    k.NBLK = k.TQ // k.TT
    k.hTd = nc.dram_tensor("hTd", [k.NBLK, 128, 32, k.TT], BF16).ap()
    k.yTd = nc.dram_tensor("yTd", [k.NBLK, 128, 64, k.TT], BF16).ap()
    if stop_after == 2:
        k.dbgY = nc.dram_tensor("dbgY", [SR, 2048], BF16, kind="ExternalOutput").ap()
    with ExitStack() as semstack:
        phase1(k, semstack)
        if stop_after != 1:
            phase2(k, semstack)
            if stop_after == 2:
                S = Sched(nc)
                with ExitStack() as st:
                    S.op('sp', f_dma(k.dbgY, k.Y), writes=['dbg'], dma=True)
                    S.barrier()
                    S.emit(semstack, st)
                return nc
            phase3(k, semstack)
        if stop_after == 1:
            S = Sched(nc)
            with ExitStack() as st:
                S.op('sp', f_dma(k.dbgP, k.P[0:64 + k.NT, :]), writes=['dbg'], dma=True)
                S.barrier()
                S.emit(semstack, st)
            return nc
    return nc


def build_p2(NCH):
    k = K()
    nc = bass.Bass("TRN2", target_bir_lowering=False)
    k.nc = nc
    k.NCH = NCH
    k.NT = 64 * (NCH + 1)
    SR = 64 * NCH
    dt_in = lambda n, sh: nc.dram_tensor(n, sh, F32, kind="ExternalInput").ap()
    k.P = PSplit(dt_in("P1", [64 + k.NT + 64, 4096]), dt_in("P2", [64 + k.NT + 64, W1C - 4096]))
    k.b_ig = dt_in("b_ig", [2])
    k.b_fg = dt_in("b_fg", [2])
    k.ml_g = dt_in("ml_g", [1024])
    k.conv_w = dt_in("conv_w", [4, 3072])
    k.a_log = dt_in("a_log", [8])
    k.dt_bias = dt_in("dt_bias", [8])
    k.gdn_g = dt_in("gdn_g", [128])
    k.Y = nc.dram_tensor("Y", [SR, 2048], BF16).ap()
    k.dbgY = nc.dram_tensor("dbgY", [SR, 2048], BF16, kind="ExternalOutput").ap()
    with ExitStack() as semstack:
        phase2(k, semstack)
        S = Sched(nc)
        with ExitStack() as st:
            S.op('sp', f_dma(k.dbgY, k.Y), writes=['dbg'], dma=True)
            S.barrier()
            S.emit(semstack, st)
    return nc


def build_p3(NCH, groups):
    k = K()
    nc = bass.Bass("TRN2", target_bir_lowering=False)
    k.nc = nc
    k.NCH = NCH
    k.NT = 64 * (NCH + 1)
    SR = 64 * NCH
    k.TQ = SR // 4
    k.groups = groups
    dt_in = lambda n, sh: nc.dram_tensor(n, sh, F32, kind="ExternalInput").ap()
    k.norm_in_g = dt_in("norm_in_g", [D])
    NOW = bool(os.environ.get('P3_NOW'))
    if not NOW:
        k.wg = dt_in("wg", [D, 8192]); k.wa = dt_in("wa", [D, D]); k.wb = dt_in("wb", [D, D]); k.wo = dt_in("wo", [D, D])
    k.norm_f_g = dt_in("norm_f_g", [D])
    k.xq = dt_in("xq", [k.TQ, D])
    k.yrows = nc.dram_tensor("yrows", [128, (k.TQ // 128) * 4], mybir.dt.int32, kind="ExternalInput").ap()
    Yin = nc.dram_tensor("Yin", [SR, 2048], BF16, kind="ExternalInput").ap()
    k.out = nc.dram_tensor("out", [k.TQ, D], F32, kind="ExternalOutput").ap()
    k.wgb = nc.dram_tensor("wgb", [64, 128, 32, 128], BF16).ap()
    k.wab = nc.dram_tensor("wab", [32, 128, 32, 128], BF16).ap()
    k.wbb = nc.dram_tensor("wbb", [32, 128, 32, 128], BF16).ap()
    k.wob = nc.dram_tensor("wob", [8, 128, 32, 512], BF16).ap()
    k.Y = nc.dram_tensor("Y", [SR, 2048], BF16).ap()
    k.YG = nc.dram_tensor("YG", [4 * SR, 2048], BF16).ap()
    k.TT = min(512, k.TQ)
    k.NBLK = k.TQ // k.TT
    k.hTd = nc.dram_tensor("hTd", [k.NBLK, 128, 32, k.TT], BF16).ap()
    k.yTd = nc.dram_tensor("yTd", [k.NBLK, 128, 64, k.TT], BF16).ap()
    with ExitStack() as semstack:
        S = Sched(nc)
        with ExitStack() as st:
            S.op('sp', f_dma(k.Y, Yin), writes=['Y'], dma=True)
            for c in range(SR // 256):
                emit_exchange(k, S, c, ['Y'])
            if not NOW:
                emit_casts(k, S, ('wg', 'wa', 'wb', 'wo'))
            S.barrier()
            S.emit(semstack, st)
        if P3S >= 1:
            phase3(k, semstack)
    return nc


W_IN_OFFS = dict(mq=0, mk=2048, mv=4096, mo=8192, mz=12288, mi=16384, mf=16392, gqkv=16400, gz=28688, ga=32784,
                 gb=32816, gate_a=32848, gate_b=36944)


def yrows_table(g, TQ):
    nt = TQ // 128
    tab = np.empty((128, nt * 4), np.int32)
    for ti in range(nt):
        tg = g * TQ + ti * 128 + np.arange(128)
        for r in range(4):
            tab[:, ti * 4 + r] = (tg // 256) * 1024 + r * 256 + tg % 256
    return tab


def core_inputs(c, NCH, x, meta, norm_in_g, w_in, b_igate, b_fgate, ml_norm_g, conv_w, a_log, dt_bias, gdn_norm_g,
                w_proj_a, w_proj_b, w_out, norm_f_g, shared):
    b, g = c // 4, c % 4
    O = W_IN_OFFS
    wi = w_in[0]
    cols = np.concatenate([
        np.arange(O['mq'] + 512 * g, O['mq'] + 512 * (g + 1)),
        np.arange(O['mk'] + 512 * g, O['mk'] + 512 * (g + 1)),
        np.arange(O['mv'] + 1024 * g, O['mv'] + 1024 * (g + 1)),
        np.arange(O['mo'] + 1024 * g, O['mo'] + 1024 * (g + 1)),
        np.arange(O['mz'] + 1024 * g, O['mz'] + 1024 * (g + 1)),
        np.arange(O['gqkv'] + 1024 * g, O['gqkv'] + 1024 * (g + 1)),
        np.arange(O['gqkv'] + 4096 + 1024 * g, O['gqkv'] + 4096 + 1024 * (g + 1)),
        np.arange(O['gqkv'] + 8192 + 1024 * g, O['gqkv'] + 8192 + 1024 * (g + 1)),
        np.arange(O['gz'] + 1024 * g, O['gz'] + 1024 * (g + 1)),
        np.arange(O['mi'] + 2 * g, O['mi'] + 2 * (g + 1)),
        np.arange(O['mf'] + 2 * g, O['mf'] + 2 * (g + 1)),
        np.arange(O['ga'] + 8 * g, O['ga'] + 8 * (g + 1)),
        np.arange(O['gb'] + 8 * g, O['gb'] + 8 * (g + 1)),
    ])
    ccols = np.concatenate([np.arange(1024 * g, 1024 * (g + 1)), np.arange(4096 + 1024 * g, 4096 + 1024 * (g + 1)),
                            np.arange(8192 + 1024 * g, 8192 + 1024 * (g + 1))])
    SR = 64 * NCH
    return {
        "x": np.ascontiguousarray(x[b, :SR]),
        "meta": shared["meta"],
        "norm_in_g": shared["norm_in_g"],
        "w1": np.ascontiguousarray(wi[:, cols]),
        "wg": shared["wg"],
        "wa": shared["wa"], "wb": shared["wb"], "wo": shared["wo"],
        "b_ig": np.ascontiguousarray(b_igate[0, 2 * g:2 * g + 2]),
        "b_fg": np.ascontiguousarray(b_fgate[0, 2 * g:2 * g + 2]),
        "ml_g": np.ascontiguousarray(ml_norm_g[0, 1024 * g:1024 * (g + 1)]),
        "conv_w": np.ascontiguousarray(conv_w[0][:, ccols]),
        "a_log": np.ascontiguousarray(a_log[0, 8 * g:8 * g + 8]),
        "dt_bias": np.ascontiguousarray(dt_bias[0, 8 * g:8 * g + 8]),
        "gdn_g": shared["gdn_g"],
        "norm_f_g": shared["norm_f_g"],
        "xq": np.ascontiguousarray(x[b, g * (SR // 4):(g + 1) * (SR // 4)]),
        "yrows": yrows_table(g, SR // 4),
    }


def make_in_maps(NCH, **inp):
    inp = {k_: np.asarray(v, dtype=np.float32) for k_, v in inp.items()}
    O = W_IN_OFFS
    shared = {
        "meta": np.ascontiguousarray(inp["meta"]),
        "norm_in_g": np.ascontiguousarray(inp["norm_in_g"][0]),
        "wg": np.ascontiguousarray(inp["w_in"][0][:, O['gate_a']:O['gate_a'] + 8192]),
        "wa": np.ascontiguousarray(inp["w_proj_a"][0]),
        "wb": np.ascontiguousarray(inp["w_proj_b"][0]),
        "wo": np.ascontiguousarray(inp["w_out"][0]),
        "gdn_g": np.ascontiguousarray(inp["gdn_norm_g"][0]),
        "norm_f_g": np.ascontiguousarray(inp["norm_f_g"]),
    }
    return [core_inputs(c, NCH, shared=shared, **inp) for c in range(8)]


def run(NCH, stop_after=None, **inp):
    nc = build(NCH, stop_after)
    in_maps = make_in_maps(NCH, **inp)
    res = run_bass_kernel_spmd(nc, in_maps, core_ids=list(range(8)))
    return res.results


def kernel(**inputs):
    NCH = 128
    res = run(NCH, **inputs)
    SR = 64 * NCH
    TQ = SR // 4
    out = np.empty((2, SR, D), dtype=np.float32)
    for c in range(8):
        b, g = c // 4, c % 4
        out[b, g * TQ:(g + 1) * TQ] = res[c]["out"]
    return out


def phase2(k, semstack):
    nc = k.nc
    NT = k.NT
    NTL = (NT + 127) // 128
    S = Sched(nc)
    op = S.op
    with ExitStack() as st:
        sb = lambda n, sh, dt=F32: st.enter_context(nc.sbuf_tensor(_un(n), sh, dt))
        ps = lambda n, sh, dt=F32: st.enter_context(nc.psum_tensor(_un(n), sh, dt))
        make_consts(k, S, st)
        identf, identb = k.identf, k.identb
        onesf = sb("onesf", [128, 128])
        onesb = sb("onesb", [128, 2], BF16)
        Ubd = sb("Ubd", [128, 128]); E0 = sb("E0", [128, 128]); E1 = sb("E1", [128, 128]); Eown = sb("Eown", [128, 128])
        MASKI = sb("MASKI", [128, 128]); MASKS = sb("MASKS", [128, 128])
        op('dve', f_memset(onesf[:], 1.0), writes=['onesf'])
        op('dve', f_memset(onesb[:], 1.0), writes=['onesb'])
        op('pool', f_memset(Ubd[:], 1.0), writes=['Ubd'])
        op('pool', f_asel(Ubd[:], Ubd[:], [[1, 128]], ALU.is_ge, 0.0, 0, -1), reads=['Ubd'], writes=['Ubd'])
        op('pool', f_memset(Ubd[0:64, 64:128], 0.0), reads=['Ubd'], writes=['Ubd'])
        op('pool', f_memset(MASKI[:], 0.0), writes=['MASKI'])
        op('pool', f_asel(MASKI[:], MASKI[:], [[1, 128]], ALU.is_ge, NEG, 0, -1), reads=['MASKI'], writes=['MASKI'])
        op('pool', f_memset(MASKI[0:64, 64:128], NEG), reads=['MASKI'], writes=['MASKI'])
        op('pool', f_memset(MASKS[:], 0.0), writes=['MASKS'])
        op('pool', f_asel(MASKS[:], MASKS[:], [[1, 128]], ALU.is_ge, NEG, -1, -1), reads=['MASKS'], writes=['MASKS'])
        op('pool', f_memset(MASKS[0:64, 64:128], NEG), reads=['MASKS'], writes=['MASKS'])
        op('pool', f_memset(E0[:], 1.0), writes=['E0'])
        op('pool', f_asel(E0[:], E0[:], [[0, 128]], ALU.is_equal, 0.0, -63, 1), reads=['E0'], writes=['E0'])
        op('pool', f_memset(E1[:], 1.0), writes=['E1'])
        op('pool', f_asel(E1[:], E1[:], [[0, 128]], ALU.is_equal, 0.0, -127, 1), reads=['E1'], writes=['E1'])
        op('pool', f_copy(Eown[:, 0:64], E0[:, 0:64]), reads=['E0'], writes=['Eown'])
        op('pool', f_copy(Eown[:, 64:128], E1[:, 64:128]), reads=['E1', 'Eown'], writes=['Eown'])
        big = sb("b_ig", [128, 2]); bfg = sb("b_fg", [128, 2]); alg = sb("alg", [128, 8]); dtb = sb("dtb", [128, 8])
        mlg = sb("mlg", [128, 1024]); gdng = sb("gdng", [128, 128])
        wconv = sb("wconv", [128, 4, 3072])
        op('sp', f_dma(big[:], k.b_ig.partition_broadcast(128)), writes=['big'], dma=True)
        op('sp', f_dma(bfg[:], k.b_fg.partition_broadcast(128)), writes=['bfg'], dma=True)
        op('sp', f_dma(alg[:], k.a_log.partition_broadcast(128)), writes=['alg'], dma=True)
        op('sp', f_dma(dtb[:], k.dt_bias.partition_broadcast(128)), writes=['dtb'], dma=True)
        op('sp', f_dma(mlg[:], k.ml_g.partition_broadcast(128)), writes=['mlg'], dma=True)
        op('sp', f_dma(gdng[:], k.gdn_g.partition_broadcast(128)), writes=['gdng'], dma=True)
        for kk in range(4):
            op('sp', f_dma(wconv[:, kk, :], k.conv_w[kk].partition_broadcast(128)), writes=[('wconv', kk)], dma=True)
        RG8 = ['gdng']
        RWC = [('wconv', kk) for kk in range(4)]
        PA = ps("PA", [128, 8, 256]); PB = ps("PB", [128, 8, 128])
        PT = [ps(f"PT{i}", [128, 8, 128], BF16) for i in range(2)]
        kPA = lambda h: ('PA', h // 2)
        kPB = lambda h: ('PB', h // 4)
        KPA = [('PA', b) for b in range(4)]
        KPB = [('PB', b) for b in range(2)]

        SM = sb("SM", [128, NTL, 20])
        op('dve', f_memset(SM[:], 0.0), writes=['SM'])
        nfull = NT // 128
        for t0 in range(0, nfull, 16):
            n = min(16, nfull - t0)
            in_ = k.P[64 + 128 * t0:64 + 128 * (t0 + n), C_SM:C_SM + 20].rearrange("(n p) c -> p n c", p=128)
            op('sp', f_dma(SM[:, t0:t0 + n, :], in_), reads=['SM'], writes=['SM'], dma=True)
        if NT % 128:
            op('sp', f_dma(SM[0:64, NTL - 1, :], k.P[64 + 128 * nfull:64 + 128 * nfull + 64, C_SM:C_SM + 20]),
               reads=['SM'], writes=['SM'], dma=True)
        A2 = lambda n: sb(n, [128, NTL, 2])
        A8 = lambda n: sb(n, [128, NTL, 8])
        mI = A2("mI"); mLF = A2("mLF"); mF = A2("mF"); mGo = A2("mGo"); mG0 = A2("mG0"); mG1 = A2("mG1")
        mBI = A2("mBI"); mEF = A2("mEF"); mA = A2("mA"); mT = A2("mT")
        bi15 = sb("bi15", [128, 2]); bf15 = sb("bf15", [128, 2])
        op('dve', f_ts(bi15[:], big[:], 1.0 / 15.0, None, ALU.mult), reads=['big'], writes=['bi15'])
        op('dve', f_ts(bf15[:], bfg[:], 1.0 / 15.0, None, ALU.mult), reads=['bfg'], writes=['bf15'])
        for h in range(2):
            op('act', f_act(mI[:, :, h], SM[:, :, h], AF.Tanh, bias=bi15[:, h:h + 1], scale=1.0 / 15.0),
               reads=['SM', 'bi15'], writes=['mI'])
            op('act', f_act(mT[:, :, h], SM[:, :, 2 + h], AF.Tanh, bias=bf15[:, h:h + 1], scale=1.0 / 15.0),
               reads=['SM', 'bf15'], writes=['mT'])
        op('dve', f_ts(mI[:], mI[:], 15.0, None, ALU.mult), reads=['mI'], writes=['mI'])
        op('act', f_act(mT[:], mT[:], AF.Exp, scale=-15.0), reads=['mT'], writes=['mT'])
        op('act', f_act(mLF[:], mT[:], AF.Ln, bias=1.0), reads=['mT'], writes=['mLF'])
        op('dve', f_ts(mLF[:], mLF[:], -1.0, None, ALU.mult), reads=['mLF'], writes=['mLF'])

        def tilemm(dst, lhsT, src, lkey, skey, dkey, width):
            tot = NTL * width
            sflat = src[:].rearrange("p n w -> p (n w)")
            dflat = dst[:].rearrange("p n w -> p (n w)")
            pflat = PA[:].rearrange("p a b -> p (a b)")
            for ci, c0 in enumerate(range(0, tot, 512)):
                cw = min(512, tot - c0)
                b = ci % 4
                op('pe', f_mm(pflat[:, b * 512:b * 512 + cw], lhsT[:], sflat[:, c0:c0 + cw]), reads=[lkey, skey],
                   writes=[('PA', b)])
                op('dve', f_copy(dflat[:, c0:c0 + cw], pflat[:, b * 512:b * 512 + cw]), reads=[('PA', b)], writes=[dkey])
        tilemm(mF, Ubd, mLF, 'Ubd', 'mLF', 'mF', 2)
        tilemm(mGo, Eown, mF, 'Eown', 'mF', 'mGo', 2)
        tilemm(mG0, E0, mF, 'E0', 'mF', 'mG0', 2)
        tilemm(mG1, E1, mF, 'E1', 'mF', 'mG1', 2)
        op('dve', f_tt(mBI[:], mI[:], mF[:], ALU.subtract), reads=['mI', 'mF'], writes=['mBI'])
        op('act', f_act(mEF[:], mF[:], AF.Exp), reads=['mF'], writes=['mEF'])
        op('dve', f_tt(mA[:], mGo[:], mBI[:], ALU.add), reads=['mGo', 'mBI'], writes=['mA'])
        op('act', f_act(mA[:], mA[:], AF.Exp, bias=float(-np.log(16.0))), reads=['mA'], writes=['mA'])
        op('act', f_act(mG0[:], mG0[:], AF.Exp), reads=['mG0'], writes=['mG0'])
        op('act', f_act(mG1[:], mG1[:], AF.Exp), reads=['mG1'], writes=['mG1'])
        gBETA = A8("gBETA"); gLNB = A8("gLNB"); gG = A8("gG"); gGC = A8("gGC"); gGo = A8("gGo"); gD0 = A8("gD0"); gD1 = A8("gD1")
        gEG = A8("gEG"); gKD = gGo; gBEG = gG; gGCB = gLNB
        nega = sb("nega", [128, 8])
        op('act', f_act(nega[:], alg[:], AF.Exp), reads=['alg'], writes=['nega'])
        op('dve', f_ts(nega[:], nega[:], -1.0, None, ALU.mult), reads=['nega'], writes=['nega'])
        bc8 = lambda t: t[:].unsqueeze(1).to_broadcast([128, NTL, 8])
        op('act', f_act(gBETA[:], SM[:, :, 12:20], AF.Sigmoid), reads=['SM'], writes=['gBETA'])
        op('act', f_act(gLNB[:], gBETA[:], AF.Ln), reads=['gBETA'], writes=['gLNB'])
        op('dve', f_tt(gG[:], SM[:, :, 4:12], bc8(dtb), ALU.add), reads=['SM', 'dtb'], writes=['gG'])
        op('act', f_act(gG[:], gG[:], AF.Exp), reads=['gG'], writes=['gG'])
        op('act', f_act(gG[:], gG[:], AF.Ln, bias=1.0), reads=['gG'], writes=['gG'])
        op('dve', f_tt(gG[:], gG[:], bc8(nega), ALU.mult), reads=['gG', 'nega'], writes=['gG'])
        tilemm(gGC, Ubd, gG, 'Ubd', 'gG', 'gGC', 8)
        tilemm(gGo, Eown, gGC, 'Eown', 'gGC', 'gGo', 8)
        tilemm(gD0, E0, gGC, 'E0', 'gGC', 'gD0', 8)
        tilemm(gD1, E1, gGC, 'E1', 'gGC', 'gD1', 8)
        op('act', f_act(gEG[:], gGC[:], AF.Exp), reads=['gGC'], writes=['gEG'])
        op('dve', f_tt(gKD[:], gGo[:], gGC[:], ALU.subtract), reads=['gGo', 'gGC'], writes=['gGo', 'gKD'])
        op('act', f_act(gKD[:], gKD[:], AF.Exp), reads=['gKD'], writes=['gKD'])
        op('dve', f_tt(gBEG[:], gBETA[:], gEG[:], ALU.mult), reads=['gBETA', 'gEG', 'gG'], writes=['gG', 'gBEG'])
        op('dve', f_tt(gGCB[:], gGC[:], gLNB[:], ALU.add), reads=['gGC', 'gLNB'], writes=['gLNB', 'gGCB'])
        op('act', f_act(gD0[:], gD0[:], AF.Exp), reads=['gD0'], writes=['gD0'])
        op('act', f_act(gD1[:], gD1[:], AF.Exp), reads=['gD1'], writes=['gD1'])
        GATES = ['mBI', 'mEF', 'mA', 'mG0', 'mG1', 'mF', 'gBETA', 'gGC', 'gEG', 'gKD', 'gBEG', 'gGCB', 'gD0', 'gD1']

        Cst = sb("Cst", [128, 4, 512]); Cb = sb("Cb", [128, 4, 512], BF16)
        nst = sb("nst", [128, 4]); nb = sb("nb", [128, 4], BF16)
        Sst = sb("Sst", [128, 8, 128]); Sb = sb("Sb", [128, 8, 128], BF16)
        op('dve', f_memset(Cst[:], 0.0), writes=[('Cst', s_) for s_ in range(4)]); op('pool', f_memset(Cb[:], 0.0), writes=[('Cb', s_) for s_ in range(4)])
        op('dve', f_memset(nst[:], 0.0), writes=['nst']); op('pool', f_memset(nb[:], 0.0), writes=['nb'])
        op('dve', f_memset(Sst[:], 0.0), writes=[('Sst', b_) for b_ in range(4)]); op('pool', f_memset(Sb[:], 0.0), writes=[('Sb', b_) for b_ in range(4)])
        MQKV = sb("MQKV", [128, 2048]); MOZ = [sb("MOZ0", [128, 2048])] * 2
        GZ = [sb("GZ0", [128, 1024])] * 2
        CV = sb("CV", [128, 4, 1024])
        QKV = sb("QKV", [128, 3072])
        mQb = sb("mQb", [128, 512], BF16); mQF = sb("mQF", [128, 512], BF16); mKb = sb("mKb", [128, 512], BF16)
        mKA = sb("mKA", [128, 512], BF16); mVb = sb("mVb", [128, 1024], BF16)
        mTr = sb("mTr", [128, 12, 128], BF16)
        DGm = sb("DGm", [128, 2, 128]); Wt = sb("Wt", [128, 2, 128]); STb = sb("STb", [128, 2, 128], BF16)
        HM = sb("HMO", [128, 1024]); den = sb("den", [128, 2]); ssm = sb("ssm", [128, 2])
        YA = sb("YA", [128, 1024], BF16); YBo = sb("YBo", [128, 1024], BF16); junk = YBo
        SQ = CV[:, 0:2, :].rearrange("p a b -> p (a b)"); ss16 = sb("ss16", [128, 16]); sc = sb("sc", [128, 5, 8])
        gQb = sb("gQb", [128, 1024], BF16); gQd = sb("gQd", [128, 1024], BF16); gKb = sb("gKb", [128, 1024], BF16)
        gKbeg = sb("gKbeg", [128, 1024], BF16); gKd = sb("gKd", [128, 1024], BF16); gVb = sb("gVb", [128, 1024], BF16)
        kT = sb("kT", [128, 8, 128], BF16); qT = sb("qT", [128, 8, 128], BF16); qdT = sb("qdT", [128, 8, 128], BF16)
        DG = sb("DGU", [128, 8, 128]); EAQ = QKV[:, 0:2048].rearrange("p (h c) -> p h c", h=8)
        ZP = CV[:, 0:2, :].rearrange("p a (h2 c b) -> p (a h2) c b", h2=4, c=2); PTs = CV[:, 2, :].rearrange("p (h b) -> p h b", h=8); Zb = sb("Zb", [128, 8, 128], BF16)
        QKm = sb("QKm", [128, 8, 128], BF16); Us = DG; wTs = sb("wTs", [128, 8, 128], BF16)
        VNb = sb("VNb", [128, 8, 128], BF16); Os = HM[:].rearrange("p (h d) -> p h d", h=8); ss8 = sb("ss8", [128, 8])
        pAf = PA[:].rearrange("p a b -> p (a b)")
        pBf = PB[:].rearrange("p a b -> p (a b)")
        bch = lambda ap8: ap8.unsqueeze(2).to_broadcast([128, 8, 128])
        v3 = lambda t: t[:].rearrange("p (h d) -> p h d", h=8)

        for p in range(NTL):
            np_ = min(128, NT - 128 * p)
            r0 = 64 + 128 * p
            nchunk = np_ // 64
            if p >= 3 and p % 2 == 1 and hasattr(k, 'YG'):
                emit_exchange(k, S, (p - 3) // 2, [(nm, q) for q in (p - 3, p - 2, p - 1) for nm in ('Ya', 'Yb')])
            mz = MOZ[p % 2]; gz = GZ[p % 2]
            kmz = ('MOZ', 0); kgz = ('GZ', 0)
            op('sp', f_dma(MQKV[0:np_, :], k.P[r0:r0 + np_, C_MQ:C_MQ + 2048]), writes=['MQKV'], dma=True)
            op('sp', f_dma(mz[0:np_, :], k.P[r0:r0 + np_, C_MO:C_MO + 2048]), writes=[kmz], dma=True)
            op('sp', f_dma(gz[0:np_, :], k.P[r0:r0 + np_, C_GZ:C_GZ + 1024]), writes=[kgz], dma=True)
            def ml_gen():
                for h in range(0 if SKIP_ML else 2):
                    q_ = MQKV[:, h * 256:(h + 1) * 256]; k_ = MQKV[:, 512 + h * 256:512 + (h + 1) * 256]
                    op('dve', f_copy(mQb[:, h * 256:(h + 1) * 256], q_), reads=['MQKV'], writes=['mQb'])
                    op('act', f_act(mQF[:, h * 256:(h + 1) * 256], q_, AF.Copy, scale=mEF[:, p, h:h + 1]), reads=['MQKV', 'mEF'], writes=['mQF'])
                    op('dve', f_copy(mKb[:, h * 256:(h + 1) * 256], k_), reads=['MQKV'], writes=['mKb'])
                    op('act', f_act(mKA[:, h * 256:(h + 1) * 256], k_, AF.Copy, scale=mA[:, p, h:h + 1]), reads=['MQKV', 'mA'], writes=['mKA'])
                op('pool', f_copy(mVb[:], MQKV[:, 1024:2048]), reads=['MQKV'], writes=['mVb'])
                yield
                srcs = []
                for h in range(2):
                    for dc in range(2):
                        srcs.append((mQb, 'mQb', h * 256 + dc * 128))
                    for dc in range(2):
                        srcs.append((mQF, 'mQF', h * 256 + dc * 128))
                    for dc in range(2):
                        srcs.append((mKb, 'mKb', h * 256 + dc * 128))
                for g0 in (0, 8):
                    j = (g0 // 8) % 2
                    grp = srcs[g0:g0 + 8]
                    for q, (t_, key, c0) in enumerate(grp):
                        op('pe', f_tr(PT[j][:, q, :], t_[:, c0:c0 + 128], identb[:]), reads=[key, 'identb'], writes=[('PT', j)])
                    op('act', f_acopy(mTr[:, g0:g0 + len(grp), :], PT[j][:, 0:len(grp), :]), reads=[('PT', j)], writes=['mTr'])
                    yield
                for h in range(2):
                    op('dve', f_ts(DGm[:, h, :], identf[:], mF[:, p, h:h + 1], None, ALU.mult), reads=['identf', 'mF'], writes=['DGm'])
                    op('pe', f_mm(PB[:, h, :], onesf[:], DGm[:, h, :], True, False), reads=['onesf', 'DGm'], writes=[('PB', 0)])
                    op('pe', f_mm(PB[:, h, :], identf[:], MASKI[:], False, True), reads=['identf', 'MASKI'], writes=[('PB', 0)])
                    op('act', f_act(Wt[:, h, :], PB[:, h, :], AF.Exp, bias=mBI[:, p, h:h + 1]), reads=[('PB', 0), 'mBI'], writes=['Wt'])
                    for dc in range(2):
                        op('pe', f_mm(PB[:, 2 + h, :], mTr[:, h * 6 + 4 + dc, :], mTr[:, h * 6 + dc, :], dc == 0, dc == 1),
                           reads=['mTr'], writes=[('PB', 0)])
                    op('dve', f_stt(STb[:, h, :], PB[:, 2 + h, :], 1.0 / 16.0, Wt[:, h, :], ALU.mult, ALU.mult),
                       reads=[('PB', 0), 'Wt'], writes=['STb'])
                    yield
                for h in range(2):
                    op('pe', f_mm(pAf[:, h * 512:(h + 1) * 512], STb[:, h, :], mVb[:, h * 512:(h + 1) * 512], True, False),
                       reads=['STb', 'mVb'], writes=[('PA', h)])
                    op('pe', f_mm(PB[:, 4, h:h + 1], STb[:, h, :], onesb[:, 0:1], True, False), reads=['STb', 'onesb'], writes=[('PB', 1)])
                ci = 0
                yield
                for c in range(nchunk):
                    r = slice(64 * c, 64 * c + 64)
                    last = (c == nchunk - 1)
                    for h in range(2):
                        for dc in range(2):
                            op('pe', f_mm(pAf[r, h * 512:(h + 1) * 512], mTr[:, h * 6 + 2 + dc, r], Cb[:, h * 2 + dc, :], False, last and dc == 1),
                               reads=['mTr', ('Cb', h * 2 + dc)], writes=[('PA', h)])
                            op('pe', f_mm(PB[r, 4, h:h + 1], mTr[:, h * 6 + 2 + dc, r], nb[:, h * 2 + dc:h * 2 + dc + 1], False, last and dc == 1),
                               reads=['mTr', 'nb'], writes=[('PB', 1)])
                    if last and nchunk == 1 or not last or True:
                        pass
                    if not last or p < NTL - 1:
                        Dc = mG0 if c == 0 else mG1
                        kD = 'mG0' if c == 0 else 'mG1'
                        for h in range(2):
                            for dc in range(2):
                                b = 2 + (ci % 2); ci += 1
                                s = h * 2 + dc
                                op('pe', f_mm(pAf[:, b * 512:(b + 1) * 512], mKA[r, h * 256 + dc * 128:h * 256 + (dc + 1) * 128], mVb[r, h * 512:(h + 1) * 512]),
                                   reads=['mKA', 'mVb'], writes=[('PA', b)])
                                op('pe', f_mm(PB[:, 5, s:s + 1], mKA[r, h * 256 + dc * 128:h * 256 + (dc + 1) * 128], onesb[r, 0:1]),
                                   reads=['mKA', 'onesb'], writes=[('PB', 1)])
                                op('dve', f_stt(Cst[:, s, :], Cst[:, s, :], Dc[:, p, h:h + 1], pAf[:, b * 512:(b + 1) * 512], ALU.mult, ALU.add),
                                   reads=[('Cst', s), kD, ('PA', b)], writes=[('Cst', s)])
                                op('act', f_acopy(Cb[:, s, :], Cst[:, s, :]), reads=[('Cst', s)], writes=[('Cb', s)])
                                op('dve', f_stt(nst[:, s:s + 1], nst[:, s:s + 1], Dc[:, p, h:h + 1], PB[:, 5, s:s + 1], ALU.mult, ALU.add),
                                   reads=['nst', kD, ('PB', 1)], writes=['nst'])
                            op('act', f_acopy(nb[:, h * 2:h * 2 + 2], nst[:, h * 2:h * 2 + 2]), reads=['nst'], writes=['nb'])
                            yield
                op('act', f_act(den[:], PB[:, 4, 0:2], AF.Abs), reads=[('PB', 1)], writes=['den'])
                op('dve', f_ts(den[:], den[:], 1.0, None, ALU.max), reads=['den'], writes=['den'])
                op('dve', f_recip(den[:], den[:]), reads=['den'], writes=['den'])
                for h in range(2):
                    op('act', f_act(HM[:, h * 512:(h + 1) * 512], pAf[:, h * 512:(h + 1) * 512], AF.Copy, scale=den[:, h:h + 1]),
                       reads=[('PA', h), 'den'], writes=['HM'])
                    op('act', f_act(junk[:, 0:512], HM[:, h * 512:(h + 1) * 512], AF.Square, accum_out=ssm[:, h:h + 1]), reads=['HM'], writes=['YBo', 'ssm'])
                op('act', f_act(ssm[:], ssm[:], AF.Sqrt, bias=EPS, scale=1.0 / 512.0), reads=['ssm'], writes=['ssm'])
                op('dve', f_recip(ssm[:], ssm[:]), reads=['ssm'], writes=['ssm'])
                yield
                for h in range(2):
                    op('dve', f_stt(HM[:, h * 512:(h + 1) * 512], HM[:, h * 512:(h + 1) * 512], ssm[:, h:h + 1], mlg[:, h * 512:(h + 1) * 512], ALU.mult, ALU.mult),
                       reads=['HM', 'ssm', 'mlg'], writes=['HM'])
                op('act', f_act(mz[:, 0:1024], mz[:, 0:1024], AF.Sigmoid), reads=[kmz], writes=[kmz])
                op('act', f_act(mz[:, 1024:2048], mz[:, 1024:2048], AF.Silu), reads=[kmz], writes=[kmz])
                op('pool', f_tt(HM[:], HM[:], mz[:, 0:1024], ALU.mult), reads=['HM', kmz], writes=['HM'])
                op('dve', f_tt(YA[:], HM[:], mz[:, 1024:2048], ALU.mult), reads=['HM', kmz], writes=['YA'])
                if p == 0:
                    if np_ == 128:
                        op('act', f_dma(k.Y[0:64, 0:1024], YA[64:128, :]), reads=['YA'], writes=[('Ya', p)], dma=True)
                else:
                    y0 = 128 * p - 64
                    op('act', f_dma(k.Y[y0:y0 + np_, 0:1024], YA[0:np_, :]), reads=['YA'], writes=[('Ya', p)], dma=True)

                yield
            GFE = 'pool' if GF_POOL else 'dve'
            def gf_gen():
                for third in range(3):
                    for kk in range(4):
                        op('sp', f_dma(CV[0:np_, kk, :], k.P[r0 - 3 + kk:r0 - 3 + kk + np_, C_GQ + 1024 * third:C_GQ + 1024 * (third + 1)]),
                           writes=[('CV', kk)], dma=True)
                    for kk in range(4):
                        op(GFE, f_tt(CV[:, kk, :], CV[:, kk, :], wconv[:, kk, 1024 * third:1024 * (third + 1)], ALU.mult),
                           reads=[('CV', kk), ('wconv', kk)], writes=[('CV', kk)])
                    op(GFE, f_tt(CV[:, 0, :], CV[:, 0, :], CV[:, 1, :], ALU.add), reads=[('CV', 0), ('CV', 1)], writes=[('CV', 0)])
                    op(GFE, f_tt(CV[:, 2, :], CV[:, 2, :], CV[:, 3, :], ALU.add), reads=[('CV', 2), ('CV', 3)], writes=[('CV', 2)])
                    op(GFE, f_tt(CV[:, 0, :], CV[:, 0, :], CV[:, 2, :], ALU.add), reads=[('CV', 0), ('CV', 2)], writes=[('CV', 0)])
                    op('act', f_act(QKV[:, 1024 * third:1024 * (third + 1)], CV[:, 0, :], AF.Silu), reads=[('CV', 0)], writes=[('QKV', third)])
                    yield
                op(GFE, f_tt(SQ[:], QKV[:, 0:2048], QKV[:, 0:2048], ALU.mult), reads=[('QKV', 0), ('QKV', 1)], writes=[('CV', 0), ('CV', 1)])
                op('dve', f_red(ss16[:], SQ[:].rearrange("p (h d) -> p h d", d=128)), reads=[('CV', 0), ('CV', 1)], writes=['ss16'])
                op('act', f_act(ss16[:], ss16[:], AF.Sqrt, bias=EPS), reads=['ss16'], writes=['ss16'])
                op('dve', f_recip(ss16[:], ss16[:]), reads=['ss16'], writes=['ss16'])
                yield
                op('dve', f_ts(sc[:, 0, :], ss16[:, 0:8], 128.0 ** -0.5, None, ALU.mult), reads=['ss16'], writes=['sc'])
                op('dve', f_tt(sc[:, 1, :], sc[:, 0, :], gEG[:, p, :], ALU.mult), reads=['sc', 'gEG'], writes=['sc'])
                op('dve', f_copy(sc[:, 2, :], ss16[:, 8:16]), reads=['ss16', 'sc'], writes=['sc'])
                op('dve', f_tt(sc[:, 3, :], ss16[:, 8:16], gBEG[:, p, :], ALU.mult), reads=['ss16', 'gBEG', 'sc'], writes=['sc'])
                op('dve', f_tt(sc[:, 4, :], ss16[:, 8:16], gKD[:, p, :], ALU.mult), reads=['ss16', 'gKD', 'sc'], writes=['sc'])
                yield
                Q3 = QKV[:, 0:1024].rearrange("p (h d) -> p h d", h=8)
                K3 = QKV[:, 1024:2048].rearrange("p (h d) -> p h d", h=8)
                V3 = QKV[:, 2048:3072].rearrange("p (h d) -> p h d", h=8)
                op('dve', f_tt(v3(gQb), Q3, bch(sc[:, 0, :]), ALU.mult), reads=[('QKV', 0), 'sc'], writes=['gQb'])
                op(GFE, f_tt(v3(gQd), Q3, bch(sc[:, 1, :]), ALU.mult), reads=[('QKV', 0), 'sc'], writes=['gQd'])
                yield
                op('dve', f_tt(v3(gKb), K3, bch(sc[:, 2, :]), ALU.mult), reads=[('QKV', 1), 'sc'], writes=['gKb'])
                op(GFE, f_tt(v3(gKbeg), K3, bch(sc[:, 3, :]), ALU.mult), reads=[('QKV', 1), 'sc'], writes=['gKbeg'])
                yield
                op(GFE, f_tt(v3(gKd), K3, bch(sc[:, 4, :]), ALU.mult), reads=[('QKV', 1), 'sc'], writes=['gKd'])
                op(GFE, f_tt(v3(gVb), V3, bch(gBETA[:, p, :]), ALU.mult), reads=[('QKV', 2), 'gBETA'], writes=['gVb'])
                yield
            gens = [ml_gen(), gf_gen()] if not SKIP_GDN else [ml_gen()]
            if not INTERLEAVE:
                for g_ in gens:
                    for _ in g_:
                        pass
                gens = []
            while gens:
                for g_ in list(gens):
                    try:
                        next(g_)
                    except StopIteration:
                        gens.remove(g_)
            if SKIP_GDN:
                continue
            if GS <= 3:
                continue
            for ti, (src, skey, dst, dkey) in enumerate(((gKb, 'gKb', kT, 'kT'), (gQb, 'gQb', qT, 'qT'), (gQd, 'gQd', qdT, 'qdT'))):
                j = ti % 2
                for h in range(8):
                    op('pe', f_tr(PT[j][:, h, :], src[:, h * 128:(h + 1) * 128], identb[:]), reads=[skey, 'identb'], writes=[('PT', j)])
                op('act', f_acopy(dst[:], PT[j][:]), reads=[('PT', j)], writes=[dkey])
            if GS <= 4:
                continue
            for half, (garr, gkey, mask, mkey) in enumerate(((gGCB, 'gGCB', MASKS, 'MASKS'), (gGC, 'gGC', MASKI, 'MASKI'))):
                op('dve', f_tt(DG[:], identf[:].unsqueeze(1).to_broadcast([128, 8, 128]), bch(garr[:, p, :]), ALU.mult),
                   reads=['identf', gkey], writes=['DG'])
                for h in range(8):
                    o_ = PA[:, h, half * 128:(half + 1) * 128]
                    op('pe', f_mm(o_, onesf[:], DG[:, h, :], True, False), reads=['onesf', 'DG'], writes=[kPA(h)])
                    op('pe', f_mm(o_, identf[:], mask[:], False, True), reads=['identf', mkey], writes=[kPA(h)])
            for hb in range(0, 8, 2):
                op('dve', f_tt(EAQ[:, hb:hb + 2, :], PA[:, hb:hb + 2, :], gGC[:, p, hb:hb + 2].unsqueeze(2).to_broadcast([128, 2, 256]), ALU.subtract), reads=[('PA', hb // 2), 'gGC'], writes=[('QKV', 0), ('QKV', 1), ('EAQ', hb // 2)])
            for hb in range(0, 8, 2):
                op('act', f_act(EAQ[:, hb:hb + 2, :], EAQ[:, hb:hb + 2, :], AF.Exp), reads=[('EAQ', hb // 2)], writes=[('EAQ', hb // 2)])
            if GS <= 5:
                continue
            for h in range(8):
                op('pe', f_mm(PB[:, h, :], kT[:, h, :], kT[:, h, :]), reads=['kT'], writes=[kPB(h)])
            for hb in range(0, 8, 4):
                op('dve', f_stt(ZP[:, hb:hb + 4, 1, :], PB[:, hb:hb + 4, :], -1.0, EAQ[:, hb:hb + 4, 0:128], ALU.mult, ALU.mult), reads=[('PB', hb // 4), ('EAQ', hb // 2), ('EAQ', hb // 2 + 1)], writes=[('ZPp', hb // 2), ('ZPp', hb // 2 + 1), ('ZPz', hb // 2), ('ZPz', hb // 2 + 1), ('CV', 0), ('CV', 1)])
            for h in range(8):
                op('pe', f_mm(PB[:, h, :], kT[:, h, :], qT[:, h, :]), reads=['kT', 'qT'], writes=[kPB(h)])
            for hb in range(0, 8, 4):
                op('dve', f_tt(QKm[:, hb:hb + 4, :], PB[:, hb:hb + 4, :], EAQ[:, hb:hb + 4, 128:256], ALU.mult), reads=[('PB', hb // 4), ('EAQ', hb // 2), ('EAQ', hb // 2 + 1), ('QKV', 0), ('QKV', 1)], writes=[('QKm', hb // 4)])
            if GS <= 6:
                continue
            for h in range(8):
                op('pe', f_mm(PB[:, h, :], ZP[:, h, 1, :], identf[:]), reads=[('ZPp', h // 2), 'identf'], writes=[kPB(h)])
            for hb in range(0, 8, 4):
                op('act', f_acopy(PTs[:, hb:hb + 4, :], PB[:, hb:hb + 4, :]), reads=[('PB', hb // 4)], writes=[('PTs', hb // 4), ('CV', 2)])
            op('dve', f_tt(ZP[:, :, 0, :], ZP[:, :, 1, :], identf[:].unsqueeze(1).to_broadcast([128, 8, 128]), ALU.add),
               reads=[('ZPp', b_) for b_ in range(4)] + ['identf'], writes=[('ZPz', b_) for b_ in range(4)])
            for h in range(8):
                op('pe', f_mm(PA[:, h, 128:256], PTs[:, h, :], ZP[:, h, 1, :]), reads=[('PTs', h // 4), ('ZPp', h // 2)], writes=[kPA(h)])
                op('pe', f_mm(PB[:, h, :], ZP[:, h, 1, :], PTs[:, h, :]), reads=[('PTs', h // 4), ('ZPp', h // 2)], writes=[kPB(h)])
            for hb in range(0, 8, 2):
                op('act', f_acopy(ZP[:, hb:hb + 2, 1, :], PA[:, hb:hb + 2, 128:256]), reads=[('PA', hb // 2)], writes=[('ZPp', hb // 2)])
            for hb in range(0, 8, 4):
                op('act', f_acopy(PTs[:, hb:hb + 4, :], PB[:, hb:hb + 4, :]), reads=[('PB', hb // 4)], writes=[('PTs', hb // 4), ('CV', 2)])
            if GS <= 7:
                continue
            for m in range(1, GM + 1):
                n = 256 if m < 5 else 128
                for h in range(8):
                    op('pe', f_mm(PA[:, h, 0:128], PTs[:, h, :], ZP[:, h, 0, :]), reads=[('PTs', h // 4), ('ZPz', h // 2)], writes=[kPA(h)])
                    if m < 5:
                        op('pe', f_mm(PA[:, h, 128:256], PTs[:, h, :], ZP[:, h, 1, :]), reads=[('PTs', h // 4), ('ZPp', h // 2)], writes=[kPA(h)])
                    if m < 5:
                        op('pe', f_mm(PB[:, h, :], ZP[:, h, 1, :], PTs[:, h, :]), reads=[('PTs', h // 4), ('ZPp', h // 2)], writes=[kPB(h)])
                if GI <= 0:
                    continue
                for hb in range(0, 8, 2):
                    op('dve', f_tt(ZP[:, hb:hb + 2, 0, :], ZP[:, hb:hb + 2, 0, :], PA[:, hb:hb + 2, 0:128], ALU.add), reads=[('ZPz', hb // 2), ('PA', hb // 2)], writes=[('ZPz', hb // 2)])
                if GI <= 1:
                    continue
                if m < 5:
                    for hb in range(0, 8, 2):
                        op(LE, (f_acopy if LE == 'act' else f_copy)(ZP[:, hb:hb + 2, 1, :], PA[:, hb:hb + 2, 128:256]), reads=[('PA', hb // 2)], writes=[('ZPp', hb // 2)])
                    for hb in range(0, 8, 4):
                        op(LE, (f_acopy if LE == 'act' else f_copy)(PTs[:, hb:hb + 4, :], PB[:, hb:hb + 4, :]), reads=[('PB', hb // 4)], writes=[('PTs', hb // 4)])
            for hb in range(0, 8, 2):
                op('act' if hb % 4 else 'dve', (f_acopy if hb % 4 else f_copy)(Zb[:, hb:hb + 2, :], ZP[:, hb:hb + 2, 0, :]), reads=[('ZPz', hb // 2), ('ZPp', hb // 2), ('CV', 0), ('CV', 1), ('CV', 2)], writes=[('Zb', hb // 2)])
            if GS <= 8:
                continue
            for h in range(8):
                op('pe', f_mm(PB[:, h, :], Zb[:, h, :], gVb[:, h * 128:(h + 1) * 128]), reads=[('Zb', h // 2), 'gVb'], writes=[kPB(h)])
                op('pe', f_mm(PA[:, h, 0:128], gKbeg[:, h * 128:(h + 1) * 128], Zb[:, h, :]), reads=[('Zb', h // 2), 'gKbeg'], writes=[kPA(h)])
            for hb in range(0, 8, 4):
                op('act', f_acopy(Us[:, hb:hb + 4, :], PB[:, hb:hb + 4, :]), reads=[('PB', hb // 4)], writes=['DG'])
            for hb in range(0, 8, 2):
                op('dve', f_copy(wTs[:, hb:hb + 2, :], PA[:, hb:hb + 2, 0:128]), reads=[('PA', hb // 2)], writes=[('wTs', hb // 2)])
            if GS <= 9:
                continue
            for c in range(nchunk):
                r = slice(64 * c, 64 * c + 64)
                for h in range(8):
                    op('pe', f_mm(PA[r, h, 128:256], wTs[:, h, r], Sb[:, h, :]), reads=[('wTs', h // 2), ('Sb', h // 2)], writes=[kPA(h)])
                for hb in range(0, 8, 2):
                    op('dve', f_tt(VNb[r, hb:hb + 2, :], Us[r, hb:hb + 2, :], PA[r, hb:hb + 2, 128:256], ALU.subtract), reads=['DG', ('PA', hb // 2)], writes=[('VNb', hb // 2)])
                for h in range(8):
                    op('pe', f_mm(PB[r, h, :], qdT[:, h, r], Sb[:, h, :], True, False), reads=['qdT', ('Sb', h // 2)], writes=[kPB(h)])
                    op('pe', f_mm(PB[r, h, :], QKm[r, h, r], VNb[r, h, :], False, True), reads=[('QKm', h // 4), ('VNb', h // 2)], writes=[kPB(h)])
                if c < nchunk - 1 or p < NTL - 1:
                    Dc = gD0 if c == 0 else gD1
                    kD = 'gD0' if c == 0 else 'gD1'
                    for h in range(8):
                        op('pe', f_mm(PA[:, h, 0:128], gKd[r, h * 128:(h + 1) * 128], VNb[r, h, :]), reads=['gKd', ('VNb', h // 2)], writes=[kPA(h)])
                    for hb in range(0, 8, 2):
                        kS = ('Sst', hb // 2)
                        op('dve', f_tt(Sst[:, hb:hb + 2, :], Sst[:, hb:hb + 2, :], Dc[:, p, hb:hb + 2].unsqueeze(2).to_broadcast([128, 2, 128]), ALU.mult), reads=[kS, kD], writes=[kS])
                        op('dve', f_tt(Sst[:, hb:hb + 2, :], Sst[:, hb:hb + 2, :], PA[:, hb:hb + 2, 0:128], ALU.add), reads=[kS, ('PA', hb // 2)], writes=[kS])
                        op('act', f_acopy(Sb[:, hb:hb + 2, :], Sst[:, hb:hb + 2, :]), reads=[kS], writes=[('Sb', hb // 2)])
            if GS <= 10:
                continue
            for hb in range(0, 8, 4):
                op('act', f_acopy(Os[:, hb:hb + 4, :], PB[:, hb:hb + 4, :]), reads=[('PB', hb // 4)], writes=['HM'])
            op('dve', f_tt(SQ[:, 0:1024], Os[:].rearrange("p h d -> p (h d)"), Os[:].rearrange("p h d -> p (h d)"), ALU.mult), reads=['HM'], writes=[('CV', 0), ('CV', 1)])
            op('dve', f_red(ss8[:], SQ[:, 0:1024].rearrange("p (h d) -> p h d", d=128)), reads=[('CV', 0), ('CV', 1)], writes=['ss8'])
            op('act', f_act(ss8[:], ss8[:], AF.Sqrt, bias=EPS, scale=1.0 / 128.0), reads=['ss8'], writes=['ss8'])
            op('dve', f_recip(ss8[:], ss8[:]), reads=['ss8'], writes=['ss8'])
            op('dve', f_tt(Os[:], Os[:], bch(ss8[:]), ALU.mult), reads=['HM', 'ss8'], writes=['HM'])
            op('dve', f_tt(Os[:], Os[:], gdng[:].unsqueeze(1).to_broadcast([128, 8, 128]), ALU.mult), reads=['HM'] + RG8, writes=['HM'])
            op('act', f_act(gz[:], gz[:], AF.Silu), reads=[kgz], writes=[kgz])
            op('dve', f_tt(YBo[:], Os[:].rearrange("p h d -> p (h d)"), gz[:], ALU.mult), reads=['HM', kgz], writes=['YBo'])
            if p == 0:
                if np_ == 128:
                    op('act', f_dma(k.Y[0:64, 1024:2048], YBo[64:128, :]), reads=['YBo'], writes=[('Yb', p)], dma=True)
            else:
                y0 = 128 * p - 64
                op('act', f_dma(k.Y[y0:y0 + np_, 1024:2048], YBo[0:np_, :]), reads=['YBo'], writes=[('Yb', p)], dma=True)
        if hasattr(k, 'YG'):
            pl = NTL - 1
            emit_exchange(k, S, (pl - 2) // 2, [(nm, q) for q in (pl - 2, pl - 1, pl) for nm in ('Ya', 'Yb')])
        S.barrier()
        S.emit(semstack, st)


def phase3(k, semstack):
    nc = k.nc
    SR = 64 * k.NCH
    TQ, TT, NBLK = k.TQ, k.TT, k.NBLK
    if P3S <= 1:
        return
    S = Sched(nc)
    op = S.op
    with ExitStack() as st:
        sb = lambda n, sh, dt=F32: st.enter_context(nc.sbuf_tensor(_un(n), sh, dt))
        ps = lambda n, sh, dt=F32: st.enter_context(nc.psum_tensor(_un(n), sh, dt))
        make_consts(k, S, st)
        gbc = sb("gbc3", [128, D])
        op('sp', f_dma(gbc[:], k.norm_in_g.partition_broadcast(128)), writes=['gbc'], dma=True)
        xt = [sb(f"x3{i}", [128, D]) for i in range(2)]
        hn = [sb(f"hn3{i}", [128, D], BF16) for i in range(2)]
        ss = [sb(f"ss3{i}", [128, 1]) for i in range(2)]
        rs = [sb(f"rs3{i}", [128, 1]) for i in range(2)]
        hTt = [sb(f"hTt{i}", [128, 32, 128], BF16) for i in range(2)]
        yTt = [sb(f"yTt{i}", [128, 64, 128], BF16) for i in range(2)]
        Yl = [sb(f"Yl{i}", [128, 2048], BF16) for i in range(2)]
        yidx = sb("yidx", [128, (TQ // 128) * 4], mybir.dt.int32)
        op('sp', f_dma(yidx[:], k.yrows), writes=['yidx'], dma=True)
        ptr = [ps(f"ptr3{i}", [128, 8, 128], BF16) for i in range(2)]
        tri = 0; si = 0; yi = 0
        for ti in range(TQ // 128):
            i = ti % 2
            tok = ti * 128
            blk, a = tok // TT, tok % TT
            op('sp', f_dma(xt[i][:], k.xq[tok:tok + 128, :]), writes=[('xt', i)], dma=True)
            op('act', f_act(hn[i][:], xt[i][:], AF.Square, accum_out=ss[i][:]), reads=[('xt', i)], writes=[('hn', i), ('ss', i)])
            op('act', f_act(rs[i][:], ss[i][:], AF.Sqrt, bias=EPS, scale=1.0 / D), reads=[('ss', i)], writes=[('rs', i)])
            op('dve', f_recip(rs[i][:], rs[i][:]), reads=[('rs', i)], writes=[('rs', i)])
            op('dve', f_stt(hn[i][:], xt[i][:], rs[i][:, 0:1], gbc[:], ALU.mult, ALU.mult), reads=[('xt', i), ('rs', i), 'gbc'], writes=[('hn', i)])
            for kc0 in range(0, 32, 8):
                j = tri % 2; tri += 1
                for q in range(8):
                    kc = kc0 + q
                    op('pe', f_tr(ptr[j][:, q, :], hn[i][:, kc * 128:(kc + 1) * 128], k.identb[:]), reads=[('hn', i), 'identb'], writes=[('ptr', j)])
                op('act', f_acopy(hTt[i][:, kc0:kc0 + 8, :], ptr[j][:]), reads=[('ptr', j)], writes=[('hTt', i)])
            op('pool', f_dma(k.hTd[blk, :, :, a:a + 128], hTt[i][:]), reads=[('hTt', i)], writes=[('hTd', ti)], dma=True)
            for r in range(4):
                y = yi % 2; yi += 1
                op('pool', (lambda y_, c_: lambda e: e.indirect_dma_start(
                    out=Yl[y_][:, :], out_offset=None, in_=k.YG[:, :],
                    in_offset=bass.IndirectOffsetOnAxis(ap=yidx[:, c_:c_ + 1], axis=0)))(y, ti * 4 + r),
                   reads=['yidx'], writes=[('Yl', y)], dma=True)
                for ab in range(2):
                    j = tri % 2; tri += 1
                    for q in range(8):
                        col = ab * 1024 + q * 128
                        op('pe', f_tr(ptr[j][:, q, :], Yl[y][:, col:col + 128], k.identb[:]), reads=[('Yl', y), 'identb'], writes=[('ptr', j)])
                    dst0 = ab * 32 + r * 8
                    eng = 'dve' if tri % 2 else 'act'
                    fn = f_copy if eng == 'dve' else f_acopy
                    op(eng, fn(yTt[i][:, dst0:dst0 + 8, :], ptr[j][:]), reads=[('ptr', j)], writes=[('yTt', i)])
            op('pool', f_dma(k.yTd[blk, :, :, a:a + 128], yTt[i][:]), reads=[('yTt', i)], writes=[('yTd', ti)], dma=True)
        S.barrier()
        S.emit(semstack, st)
    if P3S <= 2:
        return
    S = Sched(nc)
    op = S.op
    with ExitStack() as st:
        sb = lambda n, sh, dt=F32: st.enter_context(nc.sbuf_tensor(_un(n), sh, dt))
        ps = lambda n, sh, dt=F32: st.enter_context(nc.psum_tensor(_un(n), sh, dt))
        hT = sb("hT3", [128, 32, TT], BF16)
        yT = sb("yT3", [128, 64, TT], BF16)
        mT = sb("mT3", [128, 32, TT], BF16)
        wbuf = [sb(f"wbuf{i}", [128, 4, 32, 128], BF16) for i in range(2)]
        sga = sb("sga", [128, TT]); sgb = sb("sgb", [128, TT]); tmp = sb("tmp3", [128, TT])
        ssf = sb("ssf", [128, 4])
        pq = [ps(f"pq{i}", [128, 512]) for i in range(8)]
        NTT = TT // 128
        gfb = yT[:].rearrange("p a b -> p (a b)").bitcast(F32)
        if TT == 512:
            xrow = hT[:].rearrange("p a b -> p (a b)").bitcast(F32)
            orow = [xrow[:, 0:D], xrow[:, D:2 * D], gfb[:, D:2 * D], gfb[:, 2 * D:3 * D]]
            obase = ['hT', 'hT', 'yT', 'yT']
        else:
            orow_sb = sb("orow", [128, NTT, D])
            orow = [orow_sb[:, tt, :] for tt in range(NTT)]
            obase = [('orowb', tt) for tt in range(NTT)]
        mTf = mT[:].rearrange("p a b -> p (a b)")[:, 0:D]
        wi = 0; pi = 0
        for blk in range(NBLK):
            op('sp', f_dma(hT[:], k.hTd[blk]), writes=['hT'], dma=True)
            op('sp', f_dma(yT[:], k.yTd[blk]), writes=['yT'], dma=True)
            for mc in range(32):
                w = wi % 2; wi += 1
                for q, (src, key, idx) in enumerate(((k.wgb, 'wgb', mc), (k.wgb, 'wgb', 32 + mc), (k.wab, 'wab', mc), (k.wbb, 'wbb', mc))):
                    op('sp', f_dma(wbuf[w][:, q, :, :], src[idx]), writes=[('wbuf', w, q)], dma=True)
                pp = [pq[(pi + q) % 8] for q in range(4)]
                kk_ = [('pq', (pi + q) % 8) for q in range(4)]
                pi += 4
                for q in range(4):
                    for kc in range(32):
                        rhs = hT[:, kc, :] if q < 2 else yT[:, (q - 2) * 32 + kc, :]
                        op('pe', f_mm(pp[q][:, 0:TT], wbuf[w][:, q, kc, :], rhs, kc == 0, kc == 31),
                           reads=[('wbuf', w, q), 'hT' if q < 2 else 'yT'], writes=[kk_[q]])
                op('act', f_act(sga[:], pp[0][:, 0:TT], AF.Sigmoid), reads=[kk_[0]], writes=['sga'])
                op('act', f_act(sgb[:], pp[1][:, 0:TT], AF.Sigmoid), reads=[kk_[1]], writes=['sgb'])
                op('dve', f_tt(sga[:], sga[:], pp[2][:, 0:TT], ALU.mult), reads=['sga', kk_[2]], writes=['sga'])
                op('dve', f_tt(sgb[:], sgb[:], pp[3][:, 0:TT], ALU.mult), reads=['sgb', kk_[3]], writes=['sgb'])
                op('pool', f_tt(mT[:, mc, :], sga[:], sgb[:], ALU.add), reads=['sga', 'sgb'], writes=['mT'])
            op('sp', f_dma(gfb[:, 0:D], k.norm_f_g.partition_broadcast(128)), writes=['yT'], dma=True)
            for tt in range(NTT):
                tok = blk * TT + tt * 128
                op('sp', f_dma(orow[tt], k.xq[tok:tok + 128, :]), writes=[obase[tt], ('orow', tt)], dma=True)
            for cg in range(8):
                w = wi % 2; wi += 1
                wo_t = wbuf[w][:].rearrange("p a b c -> p (a b c)").rearrange("p (m n) -> p m n", n=512)
                op('sp', f_dma(wo_t, k.wob[cg]), writes=[('wbuf', w, q) for q in range(4)], dma=True)
                for tt in range(NTT):
                    b = pi % 8; pi += 1
                    for mc in range(32):
                        op('pe', f_mm(pq[b][:], mT[:, mc, tt * 128:(tt + 1) * 128], wo_t[:, mc, :], mc == 0, mc == 31),
                           reads=['mT'] + [('wbuf', w, q) for q in range(4)], writes=[('pq', b)])
                    op('dve', f_tt(orow[tt][:, cg * 512:(cg + 1) * 512], orow[tt][:, cg * 512:(cg + 1) * 512], pq[b][:], ALU.add),
                       reads=[('orow', tt), ('pq', b), obase[tt]], writes=[('orow', tt)])
            for tt in range(NTT):
                tok = blk * TT + tt * 128
                op('act', f_act(mTf, orow[tt], AF.Square, accum_out=ssf[:, tt:tt + 1]), reads=[('orow', tt)], writes=['mT', ('ssf', tt)])
                op('act', f_act(ssf[:, tt:tt + 1], ssf[:, tt:tt + 1], AF.Sqrt, bias=EPS, scale=1.0 / D), reads=[('ssf', tt)], writes=[('ssf', tt)])
                op('dve', f_recip(ssf[:, tt:tt + 1], ssf[:, tt:tt + 1]), reads=[('ssf', tt)], writes=[('ssf', tt)])
                op('dve', f_stt(orow[tt], orow[tt], ssf[:, tt:tt + 1], gfb[:, 0:D], ALU.mult, ALU.mult),
                   reads=[('orow', tt), ('ssf', tt), 'yT'], writes=[('orow', tt)])
                op('act', f_dma(k.out[tok:tok + 128, :], orow[tt]), reads=[('orow', tt), obase[tt]], writes=[('out', tok)], dma=True)
        S.barrier()
        S.emit(semstack, st)
```

```python
import numpy as np
from contextlib import ExitStack
import concourse.bass as bass
import concourse.mybir as mybir
from concourse.bass_utils import run_bass_kernel_spmd

F32 = mybir.dt.float32
BF16 = mybir.dt.bfloat16
AF = mybir.ActivationFunctionType
ALU = mybir.AluOpType
AX = mybir.AxisListType

D = 4096
EPS = 1e-6
NEG = -30000.0
import os
SKIP_GDN = bool(int(os.environ.get('SKIP_GDN', '0')))
SKIP_ML = bool(int(os.environ.get('SKIP_ML', '0')))
GS = int(os.environ.get('GDN_STOP', '99'))
GF_POOL = bool(int(os.environ.get('GF_POOL', '0')))
INTERLEAVE = bool(int(os.environ.get('INTERLEAVE', '1')))
P3S = int(os.environ.get('P3_STOP', '99'))
GM = int(os.environ.get('GM', '5'))
GI = int(os.environ.get('GI', '9'))
LE = os.environ.get('LE', 'act')
LX = ['ZPz'] if os.environ.get('LX') else []
W1C = 8212
C_MQ, C_MK, C_MV, C_MO, C_MZ, C_GQ, C_GK, C_GV, C_GZ, C_SM = 0, 512, 1024, 2048, 3072, 4096, 5120, 6144, 7168, 8192


PSUM_KEYS = ('PA', 'PB', 'PT', 'pg', 'ptr', 'psel', 'pq')


class Sched:
    CAP = 12000
    NDMA = 8

    def __init__(self, nc, same_engine_sync=True):
        self.nc = nc
        self.same = same_engine_sync
        self.compute = ('pe', 'act', 'dve', 'pool', 'sp')
        self.ops = {e: [] for e in self.compute}
        self.seq = {}
        self.lastw = {}
        self.readers = {}
        self.dma_rr = {e: 0 for e in self.compute}

    @staticmethod
    def _unit(s):
        return 16 if isinstance(s, tuple) else 1

    def op(self, eng, fn, reads=(), writes=(), dma=False, signal=False):
        if dma:
            k = self.dma_rr[eng] % self.NDMA
            self.dma_rr[eng] += 1
            stream = (eng, k)
        else:
            stream = eng
        deps = set()
        xr = [b for b in reads if isinstance(b, tuple) and b[0] in PSUM_KEYS]
        if xr:
            writes = list(writes) + [b for b in xr if b not in writes]
        for b in reads:
            t = self.lastw.get(b)
            if t is not None:
                deps.add(t)
        for b in writes:
            t = self.lastw.get(b)
            if t is not None:
                deps.add(t)
            for t in self.readers.get(b, ()):
                deps.add(t)
        fdeps = []
        for (s, i) in deps:
            if s == eng and (eng == 'pe' or not self.same):
                continue
            fdeps.append((s, i))
        self.seq[stream] = self.seq.get(stream, 0) + 1
        tok = (stream, self.seq[stream])
        self.ops[eng].append(dict(fn=fn, deps=fdeps, tok=tok, dma=dma or signal))
        for b in writes:
            self.lastw[b] = tok
            self.readers[b] = []
        for b in reads:
            self.readers.setdefault(b, []).append(tok)
        return tok

    def barrier(self):
        toks = [(s, n) for s, n in self.seq.items()]
        for e in self.compute:
            self.ops[e].append(dict(fn=None, deps=[t for t in toks if t[0] != e], tok=None, dma=False))

    def emit(self, semstack, blockstack):
        nc = self.nc
        needed = {}
        for e in self.compute:
            for o in self.ops[e]:
                for (s, i) in o['deps']:
                    needed.setdefault(s, set()).add(i)
                if o['dma'] and o['tok'] is not None:
                    needed.setdefault(o['tok'][0], set()).add(o['tok'][1])
        rank, sems = {}, {}
        for s, st in needed.items():
            srt = sorted(st)
            rank[s] = {v: r + 1 for r, v in enumerate(srt)}
            n = (len(srt) + self.CAP - 1) // self.CAP
            nm = s if isinstance(s, str) else f"{s[0]}d{s[1]}"
            sems[s] = [semstack.enter_context(nc.semaphore(_un(f"s_{nm}_{j}"))) for j in range(n)]

        def tokval(s, i):
            r = rank[s][i]
            return (r - 1) // self.CAP, ((r - 1) % self.CAP + 1) * self._unit(s)

        block = blockstack.enter_context(nc.Block())
        engobj = {'pe': 'tensor', 'act': 'scalar', 'dve': 'vector', 'pool': 'gpsimd', 'sp': 'sync'}
        for e in self.compute:
            oplist = self.ops[e]
            if not oplist:
                continue

            def body(eng, oplist=oplist):
                seen = {}
                for o in oplist:
                    for (s, i) in o['deps']:
                        semidx, val = tokval(s, i)
                        cur = seen.get(s, (-1, 0))
                        if cur[0] > semidx or (cur[0] == semidx and cur[1] >= val):
                            continue
                        seen[s] = (semidx, val)
                        eng.wait_ge(sems[s][semidx], val)
                    if o['fn'] is None:
                        continue
                    ins = o['fn'](eng)
                    s, i = o['tok']
                    if s in rank and i in rank[s]:
                        semidx, _ = tokval(s, i)
                        ins.then_inc(sems[s][semidx], self._unit(s))

            getattr(block, engobj[e])(body)


def f_dma(out, in_):
    return lambda e: e.dma_start(out=out, in_=in_)


def f_mm(out, lhsT, rhs, start=True, stop=True):
    return lambda e: e.matmul(out, lhsT=lhsT, rhs=rhs, start=start, stop=stop)


def f_tr(out, in_, ident):
    return lambda e: e.transpose(out=out, in_=in_, identity=ident)


def f_act(out, in_, func, bias=0.0, scale=1.0, accum_out=None):
    if accum_out is None:
        return lambda e: e.activation(out=out, in_=in_, func=func, bias=bias, scale=scale)
    return lambda e: e.activation(out=out, in_=in_, func=func, bias=bias, scale=scale, accum_out=accum_out)


def f_copy(out, in_):
    return lambda e: e.tensor_copy(out=out, in_=in_)


def f_acopy(out, in_):
    return lambda e: e.copy(out=out, in_=in_)


def f_tt(out, in0, in1, op):
    return lambda e: e.tensor_tensor(out=out, in0=in0, in1=in1, op=op)


def f_ts(out, in0, s1, s2, op0, op1=None):
    if op1 is None:
        return lambda e: e.tensor_scalar(out=out, in0=in0, scalar1=s1, scalar2=None, op0=op0)
    return lambda e: e.tensor_scalar(out=out, in0=in0, scalar1=s1, scalar2=s2, op0=op0, op1=op1)


def f_stt(out, in0, scalar, in1, op0, op1):
    return lambda e: e.scalar_tensor_tensor(out=out, in0=in0, scalar=scalar, in1=in1, op0=op0, op1=op1)


def f_red(out, in_, op=None, axis=None):
    return lambda e: e.tensor_reduce(out=out, in_=in_, axis=axis or AX.X, op=op or ALU.add)


def f_memset(ap, v):
    return lambda e: e.memset(ap, v)


def f_recip(out, in_):
    return lambda e: e.reciprocal(out=out, in_=in_)


def f_asel(out, in_, pattern, op, fill, base, cm):
    return lambda e: e.affine_select(out=out, in_=in_, pattern=pattern, compare_op=op, fill=fill, base=base,
                                     channel_multiplier=cm)


_UN = [0]


def _un(n):
    _UN[0] += 1
    return f"{n}_{_UN[0]}"


class PSplit:
    def __init__(self, a, b):
        self.a, self.b = a, b

    def __getitem__(self, key):
        rows, cols = key
        c0, c1 = cols.start, cols.stop
        if c1 <= 4096:
            return self.a[rows, c0:c1]
        assert c0 >= 4096
        return self.b[rows, c0 - 4096:c1 - 4096]


class K:
    pass


def make_consts(k, S, st):
    nc = k.nc
    sb = lambda n, sh, dt=F32: st.enter_context(nc.sbuf_tensor(_un(n), sh, dt))
    k.identf = sb("identf", [128, 128])
    k.identb = sb("identb", [128, 128], BF16)
    S.op('pool', f_memset(k.identf[:], 1.0), writes=['identf'])
    S.op('pool', f_asel(k.identf[:], k.identf[:], [[-1, 128]], ALU.is_equal, 0.0, 0, 1), reads=['identf'], writes=['identf'])
    S.op('dve', f_copy(k.identb[:], k.identf[:]), reads=['identf'], writes=['identb'])


def emit_exchange(k, S, c, reads):
    CR = 256
    groups = getattr(k, 'groups', [[0, 1, 2, 3], [4, 5, 6, 7]])
    S.op('pool', lambda e: e.collective_compute("AllGather", ALU.bypass, replica_groups=groups,
                                                ins=[k.Y[c * CR:(c + 1) * CR, :]],
                                                outs=[k.YG[c * 4 * CR:(c + 1) * 4 * CR, :]]),
         reads=reads, writes=[('YG', c)], signal=True)


def emit_casts(k, S, which, defer=False):
    later = []
    def cast_w(src, dst, ncols, key, gw=512):
        ncg = (ncols + gw - 1) // gw
        for cg in range(ncg):
            c0 = cg * gw
            cw = min(gw, ncols - c0)
            for q in range(4):
                in_ = src[q * 1024:(q + 1) * 1024, c0:c0 + cw].rearrange("(kc p) n -> p kc n", p=128)
                out = dst[cg, :, q * 8:(q + 1) * 8, 0:cw]
                if defer:
                    later.append((out, in_, (key, cg, q)))
                else:
                    S.op('pool', f_dma(out, in_), writes=[(key, cg, q)], dma=True)
    if 'w1' in which:
        cast_w(k.w1, k.w1b, W1C, 'w1b')
    if 'wg' in which:
        cast_w(k.wg, k.wgb, 8192, 'wgb', 128)
    if 'wa' in which:
        cast_w(k.wa, k.wab, 4096, 'wab', 128)
    if 'wb' in which:
        cast_w(k.wb, k.wbb, 4096, 'wbb', 128)
    if 'wo' in which:
        cast_w(k.wo, k.wob, 4096, 'wob')
    return later


def phase1(k, semstack):
    nc = k.nc
    NT = k.NT
    TB = 512
    S = Sched(nc)
    with ExitStack() as st:
        sb = lambda n, sh, dt=F32: st.enter_context(nc.sbuf_tensor(_un(n), sh, dt))
        ps = lambda n, sh, dt=F32: st.enter_context(nc.psum_tensor(_un(n), sh, dt))
        make_consts(k, S, st)
        emit_casts(k, S, ('w1',))
        later = emit_casts(k, S, ('wg', 'wa', 'wb', 'wo'), defer=True)
        zt = sb("zt", [64, W1C])
        S.op('dve', f_memset(zt[:], 0.0), writes=['zt'])
        S.op('sp', f_dma(k.P[0:64, 0:4096], zt[:, 0:4096]), reads=['zt'], writes=['Pmargin'], dma=True)
        S.op('sp', f_dma(k.P[0:64, 4096:W1C], zt[:, 4096:W1C]), reads=['zt'], writes=['Pmargin2'], dma=True)

        gbc = sb("gbc", [128, D])
        S.op('sp', f_dma(gbc[:], k.norm_in_g.partition_broadcast(128)), writes=['gbc'], dma=True)
        xt = [sb(f"xt{i}", [128, D]) for i in range(2)]
        hn = [sb(f"hn{i}", [128, D], BF16) for i in range(2)]
        ss = [sb(f"ss{i}", [128, 1]) for i in range(2)]
        rs = [sb(f"rs{i}", [128, 1]) for i in range(2)]
        hT = sb("hT", [128, 32, TB], BF16)
        wt = [sb(f"wt{i}", [128, 32, 512], BF16) for i in range(2)]
        ost = [sb(f"ost{i}", [128, 512]) for i in range(4)]
        ptr = [ps(f"ptr{i}", [128, 8, 128], BF16) for i in range(2)]
        pg = [ps(f"pg{i}", [128, 512]) for i in range(4)]

        blocks = [(0, 64)]
        t = 64
        while t < NT:
            n = min(TB, NT - t)
            blocks.append((t, n))
            t += n
        xi = 0
        tri = 0
        gi = 0
        wi = 0
        ncg = (W1C + 511) // 512
        for (t0, nt) in blocks:
            tiles = [(a, min(128, nt - a)) for a in range(0, nt, 128)]
            for (a, np_) in tiles:
                i = xi % 2
                xi += 1
                tok = t0 + a
                if tok == 0:
                    S.op('dve', f_memset(xt[i][0:64, :], 0.0), writes=[('xt', i)])
                    S.op('sp', f_dma(xt[i][48:64, :], k.meta), reads=[('xt', i)], writes=[('xt', i)], dma=True)
                else:
                    S.op('sp', f_dma(xt[i][0:np_, :], k.x[tok - 64:tok - 64 + np_, :]), writes=[('xt', i)], dma=True)
                S.op('act', f_act(hn[i][0:np_, :], xt[i][0:np_, :], AF.Square, accum_out=ss[i][0:np_, :]),
                     reads=[('xt', i)], writes=[('hn', i), ('ss', i)])
                S.op('act', f_act(rs[i][0:np_, :], ss[i][0:np_, :], AF.Sqrt, bias=EPS, scale=1.0 / D),
                     reads=[('ss', i)], writes=[('rs', i)])
                S.op('dve', f_recip(rs[i][0:np_, :], rs[i][0:np_, :]), reads=[('rs', i)], writes=[('rs', i)])
                S.op('dve', f_stt(hn[i][0:np_, :], xt[i][0:np_, :], rs[i][0:np_, 0:1], gbc[0:np_, :], ALU.mult, ALU.mult),
                     reads=[('xt', i), ('rs', i), 'gbc'], writes=[('hn', i)])
                for kc0 in range(0, 32, 8):
                    j = tri % 2
                    tri += 1
                    for q in range(8):
                        kc = kc0 + q
                        S.op('pe', f_tr(ptr[j][:, q, 0:np_], hn[i][0:np_, kc * 128:(kc + 1) * 128], k.identb[0:np_, 0:np_]),
                             reads=[('hn', i), 'identb'], writes=[('ptr', j)])
                    eng = 'act' if (tri % 2) else 'dve'
                    fn = f_acopy if eng == 'act' else f_copy
                    S.op(eng, fn(hT[:, kc0:kc0 + 8, a:a + np_], ptr[j][:, :, 0:np_]), reads=[('ptr', j)], writes=['hT'])
            for cg in range(ncg):
                c0 = cg * 512
                cw = min(512, W1C - c0)
                w = wi % 2
                wi += 1
                S.op('sp', f_dma(wt[w][:, :, 0:cw], k.w1b[cg, :, :, 0:cw]), reads=[('w1b', cg, q) for q in range(4)],
                     writes=[('wt', w)], dma=True)
                for (a, np_) in tiles:
                    g = gi % 4
                    gi += 1
                    for kc in range(32):
                        S.op('pe', f_mm(pg[g][0:np_, 0:cw], hT[:, kc, a:a + np_], wt[w][:, kc, 0:cw], kc == 0, kc == 31),
                             reads=['hT', ('wt', w)], writes=[('pg', g)])
                    eng = 'act' if (gi % 2) else 'dve'
                    fn = f_acopy if eng == 'act' else f_copy
                    S.op(eng, fn(ost[g][0:np_, 0:cw], pg[g][0:np_, 0:cw]), reads=[('pg', g)], writes=[('ost', g)])
                    r0 = 64 + t0 + a
                    S.op('pool', f_dma(k.P[r0:r0 + np_, c0:c0 + cw], ost[g][0:np_, 0:cw]), reads=[('ost', g)],
                         writes=[('P', gi)], dma=True)
                sl_ = blocks.index((t0, nt)) * ncg + cg
                nsl_ = len(blocks) * ncg
                for (o_, i_, key_) in later[sl_ * len(later) // nsl_:(sl_ + 1) * len(later) // nsl_]:
                    S.op('pool', f_dma(o_, i_), writes=[key_], dma=True)
        S.barrier()
        S.emit(semstack, st)


def build(NCH, stop_after=None):
    k = K()
    nc = bass.Bass("TRN2", target_bir_lowering=False)
    k.nc = nc
    k.NCH = NCH
    k.NT = 64 * (NCH + 1)
    SR = 64 * NCH
    k.TQ = SR // 4
    dt_in = lambda n, sh: nc.dram_tensor(n, sh, F32, kind="ExternalInput").ap()
    k.x = dt_in("x", [SR, D])
    k.meta = dt_in("meta", [16, D])
    k.norm_in_g = dt_in("norm_in_g", [D])
    k.w1 = dt_in("w1", [D, W1C])
    k.wg = dt_in("wg", [D, 8192])
    k.wa = dt_in("wa", [D, D])
    k.wb = dt_in("wb", [D, D])
    k.wo = dt_in("wo", [D, D])
    k.b_ig = dt_in("b_ig", [2])
    k.b_fg = dt_in("b_fg", [2])
    k.ml_g = dt_in("ml_g", [1024])
    k.conv_w = dt_in("conv_w", [4, 3072])
    k.a_log = dt_in("a_log", [8])
    k.dt_bias = dt_in("dt_bias", [8])
    k.gdn_g = dt_in("gdn_g", [128])
    k.norm_f_g = dt_in("norm_f_g", [D])
    k.xq = dt_in("xq", [k.TQ, D])
    k.yrows = nc.dram_tensor("yrows", [128, (k.TQ // 128) * 4], mybir.dt.int32, kind="ExternalInput").ap()
    k.out = nc.dram_tensor("out", [k.TQ, D], F32, kind="ExternalOutput").ap()
    if stop_after == 1:
        k.dbgP = nc.dram_tensor("dbgP", [64 + k.NT, W1C], F32, kind="ExternalOutput").ap()
    k.P = PSplit(nc.dram_tensor("P1", [64 + k.NT + 64, 4096], F32).ap(),
                 nc.dram_tensor("P2", [64 + k.NT + 64, W1C - 4096], F32).ap())
    k.w1b = nc.dram_tensor("w1b", [17, 128, 32, 512], BF16).ap()
    k.wgb = nc.dram_tensor("wgb", [64, 128, 32, 128], BF16).ap()
    k.wab = nc.dram_tensor("wab", [32, 128, 32, 128], BF16).ap()
    k.wbb = nc.dram_tensor("wbb", [32, 128, 32, 128], BF16).ap()
    k.wob = nc.dram_tensor("wob", [8, 128, 32, 512], BF16).ap()
    k.Y = nc.dram_tensor("Y", [SR, 2048], BF16).ap()
    k.YG = nc.dram_tensor("YG", [4 * SR, 2048], BF16).ap()
    k.TT = min(512, k.TQ)
    k.NBLK = k.TQ // k.TT
    k.hTd = nc.dram_tensor("hTd", [k.NBLK, 128, 32, k.TT], BF16).ap()
    k.yTd = nc.dram_tensor("yTd", [k.NBLK, 128, 64, k.TT], BF16).ap()
    if stop_after == 2:
        k.dbgY = nc.dram_tensor("dbgY", [SR, 2048], BF16, kind="ExternalOutput").ap()
    with ExitStack() as semstack:
        phase1(k, semstack)
        if stop_after != 1:
            phase2(k, semstack)
            if stop_after == 2:
                S = Sched(nc)
                with ExitStack() as st:
                    S.op('sp', f_dma(k.dbgY, k.Y), writes=['dbg'], dma=True)
                    S.barrier()
                    S.emit(semstack, st)
                return nc
            phase3(k, semstack)
        if stop_after == 1:
            S = Sched(nc)
            with ExitStack() as st:
                S.op('sp', f_dma(k.dbgP, k.P[0:64 + k.NT, :]), writes=['dbg'], dma=True)
                S.barrier()
                S.emit(semstack, st)
            return nc
    return nc


def build_p2(NCH):
    k = K()
    nc = bass.Bass("TRN2", target_bir_lowering=False)
    k.nc = nc
    k.NCH = NCH
    k.NT = 64 * (NCH + 1)
    SR = 64 * NCH
    dt_in = lambda n, sh: nc.dram_tensor(n, sh, F32, kind="ExternalInput").ap()
    k.P = PSplit(dt_in("P1", [64 + k.NT + 64, 4096]), dt_in("P2", [64 + k.NT + 64, W1C - 4096]))
    k.b_ig = dt_in("b_ig", [2])
    k.b_fg = dt_in("b_fg", [2])
    k.ml_g = dt_in("ml_g", [1024])
    k.conv_w = dt_in("conv_w", [4, 3072])
    k.a_log = dt_in("a_log", [8])
    k.dt_bias = dt_in("dt_bias", [8])
    k.gdn_g = dt_in("gdn_g", [128])
    k.Y = nc.dram_tensor("Y", [SR, 2048], BF16).ap()
    k.dbgY = nc.dram_tensor("dbgY", [SR, 2048], BF16, kind="ExternalOutput").ap()
    with ExitStack() as semstack:
        phase2(k, semstack)
        S = Sched(nc)
        with ExitStack() as st:
            S.op('sp', f_dma(k.dbgY, k.Y), writes=['dbg'], dma=True)
            S.barrier()
            S.emit(semstack, st)
    return nc


def build_p3(NCH, groups):
    k = K()
    nc = bass.Bass("TRN2", target_bir_lowering=False)
    k.nc = nc
    k.NCH = NCH
    k.NT = 64 * (NCH + 1)
    SR = 64 * NCH
    k.TQ = SR // 4
    k.groups = groups
    dt_in = lambda n, sh: nc.dram_tensor(n, sh, F32, kind="ExternalInput").ap()
    k.norm_in_g = dt_in("norm_in_g", [D])
    NOW = bool(os.environ.get('P3_NOW'))
    if not NOW:
        k.wg = dt_in("wg", [D, 8192]); k.wa = dt_in("wa", [D, D]); k.wb = dt_in("wb", [D, D]); k.wo = dt_in("wo", [D, D])
    k.norm_f_g = dt_in("norm_f_g", [D])
    k.xq = dt_in("xq", [k.TQ, D])
    k.yrows = nc.dram_tensor("yrows", [128, (k.TQ // 128) * 4], mybir.dt.int32, kind="ExternalInput").ap()
    Yin = nc.dram_tensor("Yin", [SR, 2048], BF16, kind="ExternalInput").ap()
    k.out = nc.dram_tensor("out", [k.TQ, D], F32, kind="ExternalOutput").ap()
    k.wgb = nc.dram_tensor("wgb", [64, 128, 32, 128], BF16).ap()
    k.wab = nc.dram_tensor("wab", [32, 128, 32, 128], BF16).ap()
    k.wbb = nc.dram_tensor("wbb", [32, 128, 32, 128], BF16).ap()
    k.wob = nc.dram_tensor("wob", [8, 128, 32, 512], BF16).ap()
    k.Y = nc.dram_tensor("Y", [SR, 2048], BF16).ap()
    k.YG = nc.dram_tensor("YG", [4 * SR, 2048], BF16).ap()
    k.TT = min(512, k.TQ)
    k.NBLK = k.TQ // k.TT
    k.hTd = nc.dram_tensor("hTd", [k.NBLK, 128, 32, k.TT], BF16).ap()
    k.yTd = nc.dram_tensor("yTd", [k.NBLK, 128, 64, k.TT], BF16).ap()
    with ExitStack() as semstack:
        S = Sched(nc)
        with ExitStack() as st:
            S.op('sp', f_dma(k.Y, Yin), writes=['Y'], dma=True)
            for c in range(SR // 256):
                emit_exchange(k, S, c, ['Y'])
            if not NOW:
                emit_casts(k, S, ('wg', 'wa', 'wb', 'wo'))
            S.barrier()
            S.emit(semstack, st)
        if P3S >= 1:
            phase3(k, semstack)
    return nc


W_IN_OFFS = dict(mq=0, mk=2048, mv=4096, mo=8192, mz=12288, mi=16384, mf=16392, gqkv=16400, gz=28688, ga=32784,
                 gb=32816, gate_a=32848, gate_b=36944)


def yrows_table(g, TQ):
    nt = TQ // 128
    tab = np.empty((128, nt * 4), np.int32)
    for ti in range(nt):
        tg = g * TQ + ti * 128 + np.arange(128)
        for r in range(4):
            tab[:, ti * 4 + r] = (tg // 256) * 1024 + r * 256 + tg % 256
    return tab


def core_inputs(c, NCH, x, meta, norm_in_g, w_in, b_igate, b_fgate, ml_norm_g, conv_w, a_log, dt_bias, gdn_norm_g,
                w_proj_a, w_proj_b, w_out, norm_f_g, shared):
    b, g = c // 4, c % 4
    O = W_IN_OFFS
    wi = w_in[0]
    cols = np.concatenate([
        np.arange(O['mq'] + 512 * g, O['mq'] + 512 * (g + 1)),
        np.arange(O['mk'] + 512 * g, O['mk'] + 512 * (g + 1)),
        np.arange(O['mv'] + 1024 * g, O['mv'] + 1024 * (g + 1)),
        np.arange(O['mo'] + 1024 * g, O['mo'] + 1024 * (g + 1)),
        np.arange(O['mz'] + 1024 * g, O['mz'] + 1024 * (g + 1)),
        np.arange(O['gqkv'] + 1024 * g, O['gqkv'] + 1024 * (g + 1)),
        np.arange(O['gqkv'] + 4096 + 1024 * g, O['gqkv'] + 4096 + 1024 * (g + 1)),
        np.arange(O['gqkv'] + 8192 + 1024 * g, O['gqkv'] + 8192 + 1024 * (g + 1)),
        np.arange(O['gz'] + 1024 * g, O['gz'] + 1024 * (g + 1)),
        np.arange(O['mi'] + 2 * g, O['mi'] + 2 * (g + 1)),
        np.arange(O['mf'] + 2 * g, O['mf'] + 2 * (g + 1)),
        np.arange(O['ga'] + 8 * g, O['ga'] + 8 * (g + 1)),
        np.arange(O['gb'] + 8 * g, O['gb'] + 8 * (g + 1)),
    ])
    ccols = np.concatenate([np.arange(1024 * g, 1024 * (g + 1)), np.arange(4096 + 1024 * g, 4096 + 1024 * (g + 1)),
                            np.arange(8192 + 1024 * g, 8192 + 1024 * (g + 1))])
    SR = 64 * NCH
    return {
        "x": np.ascontiguousarray(x[b, :SR]),
        "meta": shared["meta"],
        "norm_in_g": shared["norm_in_g"],
        "w1": np.ascontiguousarray(wi[:, cols]),
        "wg": shared["wg"],
        "wa": shared["wa"], "wb": shared["wb"], "wo": shared["wo"],
        "b_ig": np.ascontiguousarray(b_igate[0, 2 * g:2 * g + 2]),
        "b_fg": np.ascontiguousarray(b_fgate[0, 2 * g:2 * g + 2]),
        "ml_g": np.ascontiguousarray(ml_norm_g[0, 1024 * g:1024 * (g + 1)]),
        "conv_w": np.ascontiguousarray(conv_w[0][:, ccols]),
        "a_log": np.ascontiguousarray(a_log[0, 8 * g:8 * g + 8]),
        "dt_bias": np.ascontiguousarray(dt_bias[0, 8 * g:8 * g + 8]),
        "gdn_g": shared["gdn_g"],
        "norm_f_g": shared["norm_f_g"],
        "xq": np.ascontiguousarray(x[b, g * (SR // 4):(g + 1) * (SR // 4)]),
        "yrows": yrows_table(g, SR // 4),
    }


def make_in_maps(NCH, **inp):
    inp = {k_: np.asarray(v, dtype=np.float32) for k_, v in inp.items()}
    O = W_IN_OFFS
    shared = {
        "meta": np.ascontiguousarray(inp["meta"]),
        "norm_in_g": np.ascontiguousarray(inp["norm_in_g"][0]),
        "wg": np.ascontiguousarray(inp["w_in"][0][:, O['gate_a']:O['gate_a'] + 8192]),
        "wa": np.ascontiguousarray(inp["w_proj_a"][0]),
        "wb": np.ascontiguousarray(inp["w_proj_b"][0]),
        "wo": np.ascontiguousarray(inp["w_out"][0]),
        "gdn_g": np.ascontiguousarray(inp["gdn_norm_g"][0]),
        "norm_f_g": np.ascontiguousarray(inp["norm_f_g"]),
    }
    return [core_inputs(c, NCH, shared=shared, **inp) for c in range(8)]


def run(NCH, stop_after=None, **inp):
    nc = build(NCH, stop_after)
    in_maps = make_in_maps(NCH, **inp)
    res = run_bass_kernel_spmd(nc, in_maps, core_ids=list(range(8)))
    return res.results


def kernel(**inputs):
    NCH = 128
    res = run(NCH, **inputs)
    SR = 64 * NCH
    TQ = SR // 4
    out = np.empty((2, SR, D), dtype=np.float32)
    for c in range(8):
        b, g = c // 4, c % 4
        out[b, g * TQ:(g + 1) * TQ] = res[c]["out"]
    return out


def phase2(k, semstack):
    nc = k.nc
    NT = k.NT
    NTL = (NT + 127) // 128
    S = Sched(nc)
    op = S.op
    with ExitStack() as st:
        sb = lambda n, sh, dt=F32: st.enter_context(nc.sbuf_tensor(_un(n), sh, dt))
        ps = lambda n, sh, dt=F32: st.enter_context(nc.psum_tensor(_un(n), sh, dt))
        make_consts(k, S, st)
        identf, identb = k.identf, k.identb
        onesf = sb("onesf", [128, 128])
        onesb = sb("onesb", [128, 2], BF16)
        Ubd = sb("Ubd", [128, 128]); E0 = sb("E0", [128, 128]); E1 = sb("E1", [128, 128]); Eown = sb("Eown", [128, 128])
        MASKI = sb("MASKI", [128, 128]); MASKS = sb("MASKS", [128, 128])
        op('dve', f_memset(onesf[:], 1.0), writes=['onesf'])
        op('dve', f_memset(onesb[:], 1.0), writes=['onesb'])
        op('pool', f_memset(Ubd[:], 1.0), writes=['Ubd'])
        op('pool', f_asel(Ubd[:], Ubd[:], [[1, 128]], ALU.is_ge, 0.0, 0, -1), reads=['Ubd'], writes=['Ubd'])
        op('pool', f_memset(Ubd[0:64, 64:128], 0.0), reads=['Ubd'], writes=['Ubd'])
        op('pool', f_memset(MASKI[:], 0.0), writes=['MASKI'])
        op('pool', f_asel(MASKI[:], MASKI[:], [[1, 128]], ALU.is_ge, NEG, 0, -1), reads=['MASKI'], writes=['MASKI'])
        op('pool', f_memset(MASKI[0:64, 64:128], NEG), reads=['MASKI'], writes=['MASKI'])
        op('pool', f_memset(MASKS[:], 0.0), writes=['MASKS'])
        op('pool', f_asel(MASKS[:], MASKS[:], [[1, 128]], ALU.is_ge, NEG, -1, -1), reads=['MASKS'], writes=['MASKS'])
        op('pool', f_memset(MASKS[0:64, 64:128], NEG), reads=['MASKS'], writes=['MASKS'])
        op('pool', f_memset(E0[:], 1.0), writes=['E0'])
        op('pool', f_asel(E0[:], E0[:], [[0, 128]], ALU.is_equal, 0.0, -63, 1), reads=['E0'], writes=['E0'])
        op('pool', f_memset(E1[:], 1.0), writes=['E1'])
        op('pool', f_asel(E1[:], E1[:], [[0, 128]], ALU.is_equal, 0.0, -127, 1), reads=['E1'], writes=['E1'])
        op('pool', f_copy(Eown[:, 0:64], E0[:, 0:64]), reads=['E0'], writes=['Eown'])
        op('pool', f_copy(Eown[:, 64:128], E1[:, 64:128]), reads=['E1', 'Eown'], writes=['Eown'])
        big = sb("b_ig", [128, 2]); bfg = sb("b_fg", [128, 2]); alg = sb("alg", [128, 8]); dtb = sb("dtb", [128, 8])
        mlg = sb("mlg", [128, 1024]); gdng = sb("gdng", [128, 128])
        wconv = sb("wconv", [128, 4, 3072])
        op('sp', f_dma(big[:], k.b_ig.partition_broadcast(128)), writes=['big'], dma=True)
        op('sp', f_dma(bfg[:], k.b_fg.partition_broadcast(128)), writes=['bfg'], dma=True)
        op('sp', f_dma(alg[:], k.a_log.partition_broadcast(128)), writes=['alg'], dma=True)
        op('sp', f_dma(dtb[:], k.dt_bias.partition_broadcast(128)), writes=['dtb'], dma=True)
        op('sp', f_dma(mlg[:], k.ml_g.partition_broadcast(128)), writes=['mlg'], dma=True)
        op('sp', f_dma(gdng[:], k.gdn_g.partition_broadcast(128)), writes=['gdng'], dma=True)
        for kk in range(4):
            op('sp', f_dma(wconv[:, kk, :], k.conv_w[kk].partition_broadcast(128)), writes=[('wconv', kk)], dma=True)
        RG8 = ['gdng']
        RWC = [('wconv', kk) for kk in range(4)]
        PA = ps("PA", [128, 8, 256]); PB = ps("PB", [128, 8, 128])
        PT = [ps(f"PT{i}", [128, 8, 128], BF16) for i in range(2)]
        kPA = lambda h: ('PA', h // 2)
        kPB = lambda h: ('PB', h // 4)
        KPA = [('PA', b) for b in range(4)]
        KPB = [('PB', b) for b in range(2)]

        SM = sb("SM", [128, NTL, 20])
        op('dve', f_memset(SM[:], 0.0), writes=['SM'])
        nfull = NT // 128
        for t0 in range(0, nfull, 16):
            n = min(16, nfull - t0)
            in_ = k.P[64 + 128 * t0:64 + 128 * (t0 + n), C_SM:C_SM + 20].rearrange("(n p) c -> p n c", p=128)
            op('sp', f_dma(SM[:, t0:t0 + n, :], in_), reads=['SM'], writes=['SM'], dma=True)
        if NT % 128:
            op('sp', f_dma(SM[0:64, NTL - 1, :], k.P[64 + 128 * nfull:64 + 128 * nfull + 64, C_SM:C_SM + 20]),
               reads=['SM'], writes=['SM'], dma=True)
        A2 = lambda n: sb(n, [128, NTL, 2])
        A8 = lambda n: sb(n, [128, NTL, 8])
        mI = A2("mI"); mLF = A2("mLF"); mF = A2("mF"); mGo = A2("mGo"); mG0 = A2("mG0"); mG1 = A2("mG1")
        mBI = A2("mBI"); mEF = A2("mEF"); mA = A2("mA"); mT = A2("mT")
        bi15 = sb("bi15", [128, 2]); bf15 = sb("bf15", [128, 2])
        op('dve', f_ts(bi15[:], big[:], 1.0 / 15.0, None, ALU.mult), reads=['big'], writes=['bi15'])
        op('dve', f_ts(bf15[:], bfg[:], 1.0 / 15.0, None, ALU.mult), reads=['bfg'], writes=['bf15'])
        for h in range(2):
            op('act', f_act(mI[:, :, h], SM[:, :, h], AF.Tanh, bias=bi15[:, h:h + 1], scale=1.0 / 15.0),
               reads=['SM', 'bi15'], writes=['mI'])
            op('act', f_act(mT[:, :, h], SM[:, :, 2 + h], AF.Tanh, bias=bf15[:, h:h + 1], scale=1.0 / 15.0),
               reads=['SM', 'bf15'], writes=['mT'])
        op('dve', f_ts(mI[:], mI[:], 15.0, None, ALU.mult), reads=['mI'], writes=['mI'])
        op('act', f_act(mT[:], mT[:], AF.Exp, scale=-15.0), reads=['mT'], writes=['mT'])
        op('act', f_act(mLF[:], mT[:], AF.Ln, bias=1.0), reads=['mT'], writes=['mLF'])
        op('dve', f_ts(mLF[:], mLF[:], -1.0, None, ALU.mult), reads=['mLF'], writes=['mLF'])

        def tilemm(dst, lhsT, src, lkey, skey, dkey, width):
            tot = NTL * width
            sflat = src[:].rearrange("p n w -> p (n w)")
            dflat = dst[:].rearrange("p n w -> p (n w)")
            pflat = PA[:].rearrange("p a b -> p (a b)")
            for ci, c0 in enumerate(range(0, tot, 512)):
                cw = min(512, tot - c0)
                b = ci % 4
                op('pe', f_mm(pflat[:, b * 512:b * 512 + cw], lhsT[:], sflat[:, c0:c0 + cw]), reads=[lkey, skey],
                   writes=[('PA', b)])
                op('dve', f_copy(dflat[:, c0:c0 + cw], pflat[:, b * 512:b * 512 + cw]), reads=[('PA', b)], writes=[dkey])
        tilemm(mF, Ubd, mLF, 'Ubd', 'mLF', 'mF', 2)
        tilemm(mGo, Eown, mF, 'Eown', 'mF', 'mGo', 2)
        tilemm(mG0, E0, mF, 'E0', 'mF', 'mG0', 2)
        tilemm(mG1, E1, mF, 'E1', 'mF', 'mG1', 2)
        op('dve', f_tt(mBI[:], mI[:], mF[:], ALU.subtract), reads=['mI', 'mF'], writes=['mBI'])
        op('act', f_act(mEF[:], mF[:], AF.Exp), reads=['mF'], writes=['mEF'])
        op('dve', f_tt(mA[:], mGo[:], mBI[:], ALU.add), reads=['mGo', 'mBI'], writes=['mA'])
        op('act', f_act(mA[:], mA[:], AF.Exp, bias=float(-np.log(16.0))), reads=['mA'], writes=['mA'])
        op('act', f_act(mG0[:], mG0[:], AF.Exp), reads=['mG0'], writes=['mG0'])
        op('act', f_act(mG1[:], mG1[:], AF.Exp), reads=['mG1'], writes=['mG1'])
        gBETA = A8("gBETA"); gLNB = A8("gLNB"); gG = A8("gG"); gGC = A8("gGC"); gGo = A8("gGo"); gD0 = A8("gD0"); gD1 = A8("gD1")
        gEG = A8("gEG"); gKD = gGo; gBEG = gG; gGCB = gLNB
        nega = sb("nega", [128, 8])
        op('act', f_act(nega[:], alg[:], AF.Exp), reads=['alg'], writes=['nega'])
        op('dve', f_ts(nega[:], nega[:], -1.0, None, ALU.mult), reads=['nega'], writes=['nega'])
        bc8 = lambda t: t[:].unsqueeze(1).to_broadcast([128, NTL, 8])
        op('act', f_act(gBETA[:], SM[:, :, 12:20], AF.Sigmoid), reads=['SM'], writes=['gBETA'])
        op('act', f_act(gLNB[:], gBETA[:], AF.Ln), reads=['gBETA'], writes=['gLNB'])
        op('dve', f_tt(gG[:], SM[:, :, 4:12], bc8(dtb), ALU.add), reads=['SM', 'dtb'], writes=['gG'])
        op('act', f_act(gG[:], gG[:], AF.Exp), reads=['gG'], writes=['gG'])
        op('act', f_act(gG[:], gG[:], AF.Ln, bias=1.0), reads=['gG'], writes=['gG'])
        op('dve', f_tt(gG[:], gG[:], bc8(nega), ALU.mult), reads=['gG', 'nega'], writes=['gG'])
        tilemm(gGC, Ubd, gG, 'Ubd', 'gG', 'gGC', 8)
        tilemm(gGo, Eown, gGC, 'Eown', 'gGC', 'gGo', 8)
        tilemm(gD0, E0, gGC, 'E0', 'gGC', 'gD0', 8)
        tilemm(gD1, E1, gGC, 'E1', 'gGC', 'gD1', 8)
        op('act', f_act(gEG[:], gGC[:], AF.Exp), reads=['gGC'], writes=['gEG'])
        op('dve', f_tt(gKD[:], gGo[:], gGC[:], ALU.subtract), reads=['gGo', 'gGC'], writes=['gGo', 'gKD'])
        op('act', f_act(gKD[:], gKD[:], AF.Exp), reads=['gKD'], writes=['gKD'])
        op('dve', f_tt(gBEG[:], gBETA[:], gEG[:], ALU.mult), reads=['gBETA', 'gEG', 'gG'], writes=['gG', 'gBEG'])
        op('dve', f_tt(gGCB[:], gGC[:], gLNB[:], ALU.add), reads=['gGC', 'gLNB'], writes=['gLNB', 'gGCB'])
        op('act', f_act(gD0[:], gD0[:], AF.Exp), reads=['gD0'], writes=['gD0'])
        op('act', f_act(gD1[:], gD1[:], AF.Exp), reads=['gD1'], writes=['gD1'])
        GATES = ['mBI', 'mEF', 'mA', 'mG0', 'mG1', 'mF', 'gBETA', 'gGC', 'gEG', 'gKD', 'gBEG', 'gGCB', 'gD0', 'gD1']

        Cst = sb("Cst", [128, 4, 512]); Cb = sb("Cb", [128, 4, 512], BF16)
        nst = sb("nst", [128, 4]); nb = sb("nb", [128, 4], BF16)
        Sst = sb("Sst", [128, 8, 128]); Sb = sb("Sb", [128, 8, 128], BF16)
        op('dve', f_memset(Cst[:], 0.0), writes=[('Cst', s_) for s_ in range(4)]); op('pool', f_memset(Cb[:], 0.0), writes=[('Cb', s_) for s_ in range(4)])
        op('dve', f_memset(nst[:], 0.0), writes=['nst']); op('pool', f_memset(nb[:], 0.0), writes=['nb'])
        op('dve', f_memset(Sst[:], 0.0), writes=[('Sst', b_) for b_ in range(4)]); op('pool', f_memset(Sb[:], 0.0), writes=[('Sb', b_) for b_ in range(4)])
        MQKV = sb("MQKV", [128, 2048]); MOZ = [sb("MOZ0", [128, 2048])] * 2
        GZ = [sb("GZ0", [128, 1024])] * 2
        CV = sb("CV", [128, 4, 1024])
        QKV = sb("QKV", [128, 3072])
        mQb = sb("mQb", [128, 512], BF16); mQF = sb("mQF", [128, 512], BF16); mKb = sb("mKb", [128, 512], BF16)
        mKA = sb("mKA", [128, 512], BF16); mVb = sb("mVb", [128, 1024], BF16)
        mTr = sb("mTr", [128, 12, 128], BF16)
        DGm = sb("DGm", [128, 2, 128]); Wt = sb("Wt", [128, 2, 128]); STb = sb("STb", [128, 2, 128], BF16)
        HM = sb("HMO", [128, 1024]); den = sb("den", [128, 2]); ssm = sb("ssm", [128, 2])
        YA = sb("YA", [128, 1024], BF16); YBo = sb("YBo", [128, 1024], BF16); junk = YBo
        SQ = CV[:, 0:2, :].rearrange("p a b -> p (a b)"); ss16 = sb("ss16", [128, 16]); sc = sb("sc", [128, 5, 8])
        gQb = sb("gQb", [128, 1024], BF16); gQd = sb("gQd", [128, 1024], BF16); gKb = sb("gKb", [128, 1024], BF16)
        gKbeg = sb("gKbeg", [128, 1024], BF16); gKd = sb("gKd", [128, 1024], BF16); gVb = sb("gVb", [128, 1024], BF16)
        kT = sb("kT", [128, 8, 128], BF16); qT = sb("qT", [128, 8, 128], BF16); qdT = sb("qdT", [128, 8, 128], BF16)
        DG = sb("DGU", [128, 8, 128]); EAQ = QKV[:, 0:2048].rearrange("p (h c) -> p h c", h=8)
        ZP = CV[:, 0:2, :].rearrange("p a (h2 c b) -> p (a h2) c b", h2=4, c=2); PTs = CV[:, 2, :].rearrange("p (h b) -> p h b", h=8); Zb = sb("Zb", [128, 8, 128], BF16)
        QKm = sb("QKm", [128, 8, 128], BF16); Us = DG; wTs = sb("wTs", [128, 8, 128], BF16)
        VNb = sb("VNb", [128, 8, 128], BF16); Os = HM[:].rearrange("p (h d) -> p h d", h=8); ss8 = sb("ss8", [128, 8])
        pAf = PA[:].rearrange("p a b -> p (a b)")
        pBf = PB[:].rearrange("p a b -> p (a b)")
        bch = lambda ap8: ap8.unsqueeze(2).to_broadcast([128, 8, 128])
        v3 = lambda t: t[:].rearrange("p (h d) -> p h d", h=8)

        for p in range(NTL):
            np_ = min(128, NT - 128 * p)
            r0 = 64 + 128 * p
            nchunk = np_ // 64
            if p >= 3 and p % 2 == 1 and hasattr(k, 'YG'):
                emit_exchange(k, S, (p - 3) // 2, [(nm, q) for q in (p - 3, p - 2, p - 1) for nm in ('Ya', 'Yb')])
            mz = MOZ[p % 2]; gz = GZ[p % 2]
            kmz = ('MOZ', 0); kgz = ('GZ', 0)
            op('sp', f_dma(MQKV[0:np_, :], k.P[r0:r0 + np_, C_MQ:C_MQ + 2048]), writes=['MQKV'], dma=True)
            op('sp', f_dma(mz[0:np_, :], k.P[r0:r0 + np_, C_MO:C_MO + 2048]), writes=[kmz], dma=True)
            op('sp', f_dma(gz[0:np_, :], k.P[r0:r0 + np_, C_GZ:C_GZ + 1024]), writes=[kgz], dma=True)
            def ml_gen():
                for h in range(0 if SKIP_ML else 2):
                    q_ = MQKV[:, h * 256:(h + 1) * 256]; k_ = MQKV[:, 512 + h * 256:512 + (h + 1) * 256]
                    op('dve', f_copy(mQb[:, h * 256:(h + 1) * 256], q_), reads=['MQKV'], writes=['mQb'])
                    op('act', f_act(mQF[:, h * 256:(h + 1) * 256], q_, AF.Copy, scale=mEF[:, p, h:h + 1]), reads=['MQKV', 'mEF'], writes=['mQF'])
                    op('dve', f_copy(mKb[:, h * 256:(h + 1) * 256], k_), reads=['MQKV'], writes=['mKb'])
                    op('act', f_act(mKA[:, h * 256:(h + 1) * 256], k_, AF.Copy, scale=mA[:, p, h:h + 1]), reads=['MQKV', 'mA'], writes=['mKA'])
                op('pool', f_copy(mVb[:], MQKV[:, 1024:2048]), reads=['MQKV'], writes=['mVb'])
                yield
                srcs = []
                for h in range(2):
                    for dc in range(2):
                        srcs.append((mQb, 'mQb', h * 256 + dc * 128))
                    for dc in range(2):
                        srcs.append((mQF, 'mQF', h * 256 + dc * 128))
                    for dc in range(2):
                        srcs.append((mKb, 'mKb', h * 256 + dc * 128))
                for g0 in (0, 8):
                    j = (g0 // 8) % 2
                    grp = srcs[g0:g0 + 8]
                    for q, (t_, key, c0) in enumerate(grp):
                        op('pe', f_tr(PT[j][:, q, :], t_[:, c0:c0 + 128], identb[:]), reads=[key, 'identb'], writes=[('PT', j)])
                    op('act', f_acopy(mTr[:, g0:g0 + len(grp), :], PT[j][:, 0:len(grp), :]), reads=[('PT', j)], writes=['mTr'])
                    yield
                for h in range(2):
                    op('dve', f_ts(DGm[:, h, :], identf[:], mF[:, p, h:h + 1], None, ALU.mult), reads=['identf', 'mF'], writes=['DGm'])
                    op('pe', f_mm(PB[:, h, :], onesf[:], DGm[:, h, :], True, False), reads=['onesf', 'DGm'], writes=[('PB', 0)])
                    op('pe', f_mm(PB[:, h, :], identf[:], MASKI[:], False, True), reads=['identf', 'MASKI'], writes=[('PB', 0)])
                    op('act', f_act(Wt[:, h, :], PB[:, h, :], AF.Exp, bias=mBI[:, p, h:h + 1]), reads=[('PB', 0), 'mBI'], writes=['Wt'])
                    for dc in range(2):
                        op('pe', f_mm(PB[:, 2 + h, :], mTr[:, h * 6 + 4 + dc, :], mTr[:, h * 6 + dc, :], dc == 0, dc == 1),
                           reads=['mTr'], writes=[('PB', 0)])
                    op('dve', f_stt(STb[:, h, :], PB[:, 2 + h, :], 1.0 / 16.0, Wt[:, h, :], ALU.mult, ALU.mult),
                       reads=[('PB', 0), 'Wt'], writes=['STb'])
                    yield
                for h in range(2):
                    op('pe', f_mm(pAf[:, h * 512:(h + 1) * 512], STb[:, h, :], mVb[:, h * 512:(h + 1) * 512], True, False),
                       reads=['STb', 'mVb'], writes=[('PA', h)])
                    op('pe', f_mm(PB[:, 4, h:h + 1], STb[:, h, :], onesb[:, 0:1], True, False), reads=['STb', 'onesb'], writes=[('PB', 1)])
                ci = 0
                yield
                for c in range(nchunk):
                    r = slice(64 * c, 64 * c + 64)
                    last = (c == nchunk - 1)
                    for h in range(2):
                        for dc in range(2):
                            op('pe', f_mm(pAf[r, h * 512:(h + 1) * 512], mTr[:, h * 6 + 2 + dc, r], Cb[:, h * 2 + dc, :], False, last and dc == 1),
                               reads=['mTr', ('Cb', h * 2 + dc)], writes=[('PA', h)])
                            op('pe', f_mm(PB[r, 4, h:h + 1], mTr[:, h * 6 + 2 + dc, r], nb[:, h * 2 + dc:h * 2 + dc + 1], False, last and dc == 1),
                               reads=['mTr', 'nb'], writes=[('PB', 1)])
                    if last and nchunk == 1 or not last or True:
                        pass
                    if not last or p < NTL - 1:
                        Dc = mG0 if c == 0 else mG1
                        kD = 'mG0' if c == 0 else 'mG1'
                        for h in range(2):
                            for dc in range(2):
                                b = 2 + (ci % 2); ci += 1
                                s = h * 2 + dc
                                op('pe', f_mm(pAf[:, b * 512:(b + 1) * 512], mKA[r, h * 256 + dc * 128:h * 256 + (dc + 1) * 128], mVb[r, h * 512:(h + 1) * 512]),
                                   reads=['mKA', 'mVb'], writes=[('PA', b)])
                                op('pe', f_mm(PB[:, 5, s:s + 1], mKA[r, h * 256 + dc * 128:h * 256 + (dc + 1) * 128], onesb[r, 0:1]),
                                   reads=['mKA', 'onesb'], writes=[('PB', 1)])
                                op('dve', f_stt(Cst[:, s, :], Cst[:, s, :], Dc[:, p, h:h + 1], pAf[:, b * 512:(b + 1) * 512], ALU.mult, ALU.add),
                                   reads=[('Cst', s), kD, ('PA', b)], writes=[('Cst', s)])
                                op('act', f_acopy(Cb[:, s, :], Cst[:, s, :]), reads=[('Cst', s)], writes=[('Cb', s)])
                                op('dve', f_stt(nst[:, s:s + 1], nst[:, s:s + 1], Dc[:, p, h:h + 1], PB[:, 5, s:s + 1], ALU.mult, ALU.add),
                                   reads=['nst', kD, ('PB', 1)], writes=['nst'])
                            op('act', f_acopy(nb[:, h * 2:h * 2 + 2], nst[:, h * 2:h * 2 + 2]), reads=['nst'], writes=['nb'])
                            yield
                op('act', f_act(den[:], PB[:, 4, 0:2], AF.Abs), reads=[('PB', 1)], writes=['den'])
                op('dve', f_ts(den[:], den[:], 1.0, None, ALU.max), reads=['den'], writes=['den'])
                op('dve', f_recip(den[:], den[:]), reads=['den'], writes=['den'])
                for h in range(2):
                    op('act', f_act(HM[:, h * 512:(h + 1) * 512], pAf[:, h * 512:(h + 1) * 512], AF.Copy, scale=den[:, h:h + 1]),
                       reads=[('PA', h), 'den'], writes=['HM'])
                    op('act', f_act(junk[:, 0:512], HM[:, h * 512:(h + 1) * 512], AF.Square, accum_out=ssm[:, h:h + 1]), reads=['HM'], writes=['YBo', 'ssm'])
                op('act', f_act(ssm[:], ssm[:], AF.Sqrt, bias=EPS, scale=1.0 / 512.0), reads=['ssm'], writes=['ssm'])
                op('dve', f_recip(ssm[:], ssm[:]), reads=['ssm'], writes=['ssm'])
                yield
                for h in range(2):
                    op('dve', f_stt(HM[:, h * 512:(h + 1) * 512], HM[:, h * 512:(h + 1) * 512], ssm[:, h:h + 1], mlg[:, h * 512:(h + 1) * 512], ALU.mult, ALU.mult),
                       reads=['HM', 'ssm', 'mlg'], writes=['HM'])
                op('act', f_act(mz[:, 0:1024], mz[:, 0:1024], AF.Sigmoid), reads=[kmz], writes=[kmz])
                op('act', f_act(mz[:, 1024:2048], mz[:, 1024:2048], AF.Silu), reads=[kmz], writes=[kmz])
                op('pool', f_tt(HM[:], HM[:], mz[:, 0:1024], ALU.mult), reads=['HM', kmz], writes=['HM'])
                op('dve', f_tt(YA[:], HM[:], mz[:, 1024:2048], ALU.mult), reads=['HM', kmz], writes=['YA'])
                if p == 0:
                    if np_ == 128:
                        op('act', f_dma(k.Y[0:64, 0:1024], YA[64:128, :]), reads=['YA'], writes=[('Ya', p)], dma=True)
                else:
                    y0 = 128 * p - 64
                    op('act', f_dma(k.Y[y0:y0 + np_, 0:1024], YA[0:np_, :]), reads=['YA'], writes=[('Ya', p)], dma=True)

                yield
            GFE = 'pool' if GF_POOL else 'dve'
            def gf_gen():
                for third in range(3):
                    for kk in range(4):
                        op('sp', f_dma(CV[0:np_, kk, :], k.P[r0 - 3 + kk:r0 - 3 + kk + np_, C_GQ + 1024 * third:C_GQ + 1024 * (third + 1)]),
                           writes=[('CV', kk)], dma=True)
                    for kk in range(4):
                        op(GFE, f_tt(CV[:, kk, :], CV[:, kk, :], wconv[:, kk, 1024 * third:1024 * (third + 1)], ALU.mult),
                           reads=[('CV', kk), ('wconv', kk)], writes=[('CV', kk)])
                    op(GFE, f_tt(CV[:, 0, :], CV[:, 0, :], CV[:, 1, :], ALU.add), reads=[('CV', 0), ('CV', 1)], writes=[('CV', 0)])
                    op(GFE, f_tt(CV[:, 2, :], CV[:, 2, :], CV[:, 3, :], ALU.add), reads=[('CV', 2), ('CV', 3)], writes=[('CV', 2)])
                    op(GFE, f_tt(CV[:, 0, :], CV[:, 0, :], CV[:, 2, :], ALU.add), reads=[('CV', 0), ('CV', 2)], writes=[('CV', 0)])
                    op('act', f_act(QKV[:, 1024 * third:1024 * (third + 1)], CV[:, 0, :], AF.Silu), reads=[('CV', 0)], writes=[('QKV', third)])
                    yield
                op(GFE, f_tt(SQ[:], QKV[:, 0:2048], QKV[:, 0:2048], ALU.mult), reads=[('QKV', 0), ('QKV', 1)], writes=[('CV', 0), ('CV', 1)])
                op('dve', f_red(ss16[:], SQ[:].rearrange("p (h d) -> p h d", d=128)), reads=[('CV', 0), ('CV', 1)], writes=['ss16'])
                op('act', f_act(ss16[:], ss16[:], AF.Sqrt, bias=EPS), reads=['ss16'], writes=['ss16'])
                op('dve', f_recip(ss16[:], ss16[:]), reads=['ss16'], writes=['ss16'])
                yield
                op('dve', f_ts(sc[:, 0, :], ss16[:, 0:8], 128.0 ** -0.5, None, ALU.mult), reads=['ss16'], writes=['sc'])
                op('dve', f_tt(sc[:, 1, :], sc[:, 0, :], gEG[:, p, :], ALU.mult), reads=['sc', 'gEG'], writes=['sc'])
                op('dve', f_copy(sc[:, 2, :], ss16[:, 8:16]), reads=['ss16', 'sc'], writes=['sc'])
                op('dve', f_tt(sc[:, 3, :], ss16[:, 8:16], gBEG[:, p, :], ALU.mult), reads=['ss16', 'gBEG', 'sc'], writes=['sc'])
                op('dve', f_tt(sc[:, 4, :], ss16[:, 8:16], gKD[:, p, :], ALU.mult), reads=['ss16', 'gKD', 'sc'], writes=['sc'])
                yield
                Q3 = QKV[:, 0:1024].rearrange("p (h d) -> p h d", h=8)
                K3 = QKV[:, 1024:2048].rearrange("p (h d) -> p h d", h=8)
                V3 = QKV[:, 2048:3072].rearrange("p (h d) -> p h d", h=8)
                op('dve', f_tt(v3(gQb), Q3, bch(sc[:, 0, :]), ALU.mult), reads=[('QKV', 0), 'sc'], writes=['gQb'])
                op(GFE, f_tt(v3(gQd), Q3, bch(sc[:, 1, :]), ALU.mult), reads=[('QKV', 0), 'sc'], writes=['gQd'])
                yield
                op('dve', f_tt(v3(gKb), K3, bch(sc[:, 2, :]), ALU.mult), reads=[('QKV', 1), 'sc'], writes=['gKb'])
                op(GFE, f_tt(v3(gKbeg), K3, bch(sc[:, 3, :]), ALU.mult), reads=[('QKV', 1), 'sc'], writes=['gKbeg'])
                yield
                op(GFE, f_tt(v3(gKd), K3, bch(sc[:, 4, :]), ALU.mult), reads=[('QKV', 1), 'sc'], writes=['gKd'])
                op(GFE, f_tt(v3(gVb), V3, bch(gBETA[:, p, :]), ALU.mult), reads=[('QKV', 2), 'gBETA'], writes=['gVb'])
                yield
            gens = [ml_gen(), gf_gen()] if not SKIP_GDN else [ml_gen()]
            if not INTERLEAVE:
                for g_ in gens:
                    for _ in g_:
                        pass
                gens = []
            while gens:
                for g_ in list(gens):
                    try:
                        next(g_)
                    except StopIteration:
                        gens.remove(g_)
            if SKIP_GDN:
                continue
            if GS <= 3:
                continue
            for ti, (src, skey, dst, dkey) in enumerate(((gKb, 'gKb', kT, 'kT'), (gQb, 'gQb', qT, 'qT'), (gQd, 'gQd', qdT, 'qdT'))):
                j = ti % 2
                for h in range(8):
                    op('pe', f_tr(PT[j][:, h, :], src[:, h * 128:(h + 1) * 128], identb[:]), reads=[skey, 'identb'], writes=[('PT', j)])
                op('act', f_acopy(dst[:], PT[j][:]), reads=[('PT', j)], writes=[dkey])
            if GS <= 4:
                continue
            for half, (garr, gkey, mask, mkey) in enumerate(((gGCB, 'gGCB', MASKS, 'MASKS'), (gGC, 'gGC', MASKI, 'MASKI'))):
                op('dve', f_tt(DG[:], identf[:].unsqueeze(1).to_broadcast([128, 8, 128]), bch(garr[:, p, :]), ALU.mult),
                   reads=['identf', gkey], writes=['DG'])
                for h in range(8):
                    o_ = PA[:, h, half * 128:(half + 1) * 128]
                    op('pe', f_mm(o_, onesf[:], DG[:, h, :], True, False), reads=['onesf', 'DG'], writes=[kPA(h)])
                    op('pe', f_mm(o_, identf[:], mask[:], False, True), reads=['identf', mkey], writes=[kPA(h)])
            for hb in range(0, 8, 2):
                op('dve', f_tt(EAQ[:, hb:hb + 2, :], PA[:, hb:hb + 2, :], gGC[:, p, hb:hb + 2].unsqueeze(2).to_broadcast([128, 2, 256]), ALU.subtract), reads=[('PA', hb // 2), 'gGC'], writes=[('QKV', 0), ('QKV', 1), ('EAQ', hb // 2)])
            for hb in range(0, 8, 2):
                op('act', f_act(EAQ[:, hb:hb + 2, :], EAQ[:, hb:hb + 2, :], AF.Exp), reads=[('EAQ', hb // 2)], writes=[('EAQ', hb // 2)])
            if GS <= 5:
                continue
            for h in range(8):
                op('pe', f_mm(PB[:, h, :], kT[:, h, :], kT[:, h, :]), reads=['kT'], writes=[kPB(h)])
            for hb in range(0, 8, 4):
                op('dve', f_stt(ZP[:, hb:hb + 4, 1, :], PB[:, hb:hb + 4, :], -1.0, EAQ[:, hb:hb + 4, 0:128], ALU.mult, ALU.mult), reads=[('PB', hb // 4), ('EAQ', hb // 2), ('EAQ', hb // 2 + 1)], writes=[('ZPp', hb // 2), ('ZPp', hb // 2 + 1), ('ZPz', hb // 2), ('ZPz', hb // 2 + 1), ('CV', 0), ('CV', 1)])
            for h in range(8):
                op('pe', f_mm(PB[:, h, :], kT[:, h, :], qT[:, h, :]), reads=['kT', 'qT'], writes=[kPB(h)])
            for hb in range(0, 8, 4):
                op('dve', f_tt(QKm[:, hb:hb + 4, :], PB[:, hb:hb + 4, :], EAQ[:, hb:hb + 4, 128:256], ALU.mult), reads=[('PB', hb // 4), ('EAQ', hb // 2), ('EAQ', hb // 2 + 1), ('QKV', 0), ('QKV', 1)], writes=[('QKm', hb // 4)])
            if GS <= 6:
                continue
            for h in range(8):
                op('pe', f_mm(PB[:, h, :], ZP[:, h, 1, :], identf[:]), reads=[('ZPp', h // 2), 'identf'], writes=[kPB(h)])
            for hb in range(0, 8, 4):
                op('act', f_acopy(PTs[:, hb:hb + 4, :], PB[:, hb:hb + 4, :]), reads=[('PB', hb // 4)], writes=[('PTs', hb // 4), ('CV', 2)])
            op('dve', f_tt(ZP[:, :, 0, :], ZP[:, :, 1, :], identf[:].unsqueeze(1).to_broadcast([128, 8, 128]), ALU.add),
               reads=[('ZPp', b_) for b_ in range(4)] + ['identf'], writes=[('ZPz', b_) for b_ in range(4)])
            for h in range(8):
                op('pe', f_mm(PA[:, h, 128:256], PTs[:, h, :], ZP[:, h, 1, :]), reads=[('PTs', h // 4), ('ZPp', h // 2)], writes=[kPA(h)])
                op('pe', f_mm(PB[:, h, :], ZP[:, h, 1, :], PTs[:, h, :]), reads=[('PTs', h // 4), ('ZPp', h // 2)], writes=[kPB(h)])
            for hb in range(0, 8, 2):
                op('act', f_acopy(ZP[:, hb:hb + 2, 1, :], PA[:, hb:hb + 2, 128:256]), reads=[('PA', hb // 2)], writes=[('ZPp', hb // 2)])
            for hb in range(0, 8, 4):
                op('act', f_acopy(PTs[:, hb:hb + 4, :], PB[:, hb:hb + 4, :]), reads=[('PB', hb // 4)], writes=[('PTs', hb // 4), ('CV', 2)])
            if GS <= 7:
                continue
            for m in range(1, GM + 1):
                n = 256 if m < 5 else 128
                for h in range(8):
                    op('pe', f_mm(PA[:, h, 0:128], PTs[:, h, :], ZP[:, h, 0, :]), reads=[('PTs', h // 4), ('ZPz', h // 2)], writes=[kPA(h)])
                    if m < 5:
                        op('pe', f_mm(PA[:, h, 128:256], PTs[:, h, :], ZP[:, h, 1, :]), reads=[('PTs', h // 4), ('ZPp', h // 2)], writes=[kPA(h)])
                    if m < 5:
                        op('pe', f_mm(PB[:, h, :], ZP[:, h, 1, :], PTs[:, h, :]), reads=[('PTs', h // 4), ('ZPp', h // 2)], writes=[kPB(h)])
                if GI <= 0:
                    continue
                for hb in range(0, 8, 2):
                    op('dve', f_tt(ZP[:, hb:hb + 2, 0, :], ZP[:, hb:hb + 2, 0, :], PA[:, hb:hb + 2, 0:128], ALU.add), reads=[('ZPz', hb // 2), ('PA', hb // 2)], writes=[('ZPz', hb // 2)])
                if GI <= 1:
                    continue
                if m < 5:
                    for hb in range(0, 8, 2):
                        op(LE, (f_acopy if LE == 'act' else f_copy)(ZP[:, hb:hb + 2, 1, :], PA[:, hb:hb + 2, 128:256]), reads=[('PA', hb // 2)], writes=[('ZPp', hb // 2)])
                    for hb in range(0, 8, 4):
                        op(LE, (f_acopy if LE == 'act' else f_copy)(PTs[:, hb:hb + 4, :], PB[:, hb:hb + 4, :]), reads=[('PB', hb // 4)], writes=[('PTs', hb // 4)])
            for hb in range(0, 8, 2):
                op('act' if hb % 4 else 'dve', (f_acopy if hb % 4 else f_copy)(Zb[:, hb:hb + 2, :], ZP[:, hb:hb + 2, 0, :]), reads=[('ZPz', hb // 2), ('ZPp', hb // 2), ('CV', 0), ('CV', 1), ('CV', 2)], writes=[('Zb', hb // 2)])
            if GS <= 8:
                continue
            for h in range(8):
                op('pe', f_mm(PB[:, h, :], Zb[:, h, :], gVb[:, h * 128:(h + 1) * 128]), reads=[('Zb', h // 2), 'gVb'], writes=[kPB(h)])
                op('pe', f_mm(PA[:, h, 0:128], gKbeg[:, h * 128:(h + 1) * 128], Zb[:, h, :]), reads=[('Zb', h // 2), 'gKbeg'], writes=[kPA(h)])
            for hb in range(0, 8, 4):
                op('act', f_acopy(Us[:, hb:hb + 4, :], PB[:, hb:hb + 4, :]), reads=[('PB', hb // 4)], writes=['DG'])
            for hb in range(0, 8, 2):
                op('dve', f_copy(wTs[:, hb:hb + 2, :], PA[:, hb:hb + 2, 0:128]), reads=[('PA', hb // 2)], writes=[('wTs', hb // 2)])
            if GS <= 9:
                continue
            for c in range(nchunk):
                r = slice(64 * c, 64 * c + 64)
                for h in range(8):
                    op('pe', f_mm(PA[r, h, 128:256], wTs[:, h, r], Sb[:, h, :]), reads=[('wTs', h // 2), ('Sb', h // 2)], writes=[kPA(h)])
                for hb in range(0, 8, 2):
                    op('dve', f_tt(VNb[r, hb:hb + 2, :], Us[r, hb:hb + 2, :], PA[r, hb:hb + 2, 128:256], ALU.subtract), reads=['DG', ('PA', hb // 2)], writes=[('VNb', hb // 2)])
                for h in range(8):
                    op('pe', f_mm(PB[r, h, :], qdT[:, h, r], Sb[:, h, :], True, False), reads=['qdT', ('Sb', h // 2)], writes=[kPB(h)])
                    op('pe', f_mm(PB[r, h, :], QKm[r, h, r], VNb[r, h, :], False, True), reads=[('QKm', h // 4), ('VNb', h // 2)], writes=[kPB(h)])
                if c < nchunk - 1 or p < NTL - 1:
                    Dc = gD0 if c == 0 else gD1
                    kD = 'gD0' if c == 0 else 'gD1'
                    for h in range(8):
                        op('pe', f_mm(PA[:, h, 0:128], gKd[r, h * 128:(h + 1) * 128], VNb[r, h, :]), reads=['gKd', ('VNb', h // 2)], writes=[kPA(h)])
                    for hb in range(0, 8, 2):
                        kS = ('Sst', hb // 2)
                        op('dve', f_tt(Sst[:, hb:hb + 2, :], Sst[:, hb:hb + 2, :], Dc[:, p, hb:hb + 2].unsqueeze(2).to_broadcast([128, 2, 128]), ALU.mult), reads=[kS, kD], writes=[kS])
                        op('dve', f_tt(Sst[:, hb:hb + 2, :], Sst[:, hb:hb + 2, :], PA[:, hb:hb + 2, 0:128], ALU.add), reads=[kS, ('PA', hb // 2)], writes=[kS])
                        op('act', f_acopy(Sb[:, hb:hb + 2, :], Sst[:, hb:hb + 2, :]), reads=[kS], writes=[('Sb', hb // 2)])
            if GS <= 10:
                continue
            for hb in range(0, 8, 4):
                op('act', f_acopy(Os[:, hb:hb + 4, :], PB[:, hb:hb + 4, :]), reads=[('PB', hb // 4)], writes=['HM'])
            op('dve', f_tt(SQ[:, 0:1024], Os[:].rearrange("p h d -> p (h d)"), Os[:].rearrange("p h d -> p (h d)"), ALU.mult), reads=['HM'], writes=[('CV', 0), ('CV', 1)])
            op('dve', f_red(ss8[:], SQ[:, 0:1024].rearrange("p (h d) -> p h d", d=128)), reads=[('CV', 0), ('CV', 1)], writes=['ss8'])
            op('act', f_act(ss8[:], ss8[:], AF.Sqrt, bias=EPS, scale=1.0 / 128.0), reads=['ss8'], writes=['ss8'])
            op('dve', f_recip(ss8[:], ss8[:]), reads=['ss8'], writes=['ss8'])
            op('dve', f_tt(Os[:], Os[:], bch(ss8[:]), ALU.mult), reads=['HM', 'ss8'], writes=['HM'])
            op('dve', f_tt(Os[:], Os[:], gdng[:].unsqueeze(1).to_broadcast([128, 8, 128]), ALU.mult), reads=['HM'] + RG8, writes=['HM'])
            op('act', f_act(gz[:], gz[:], AF.Silu), reads=[kgz], writes=[kgz])
            op('dve', f_tt(YBo[:], Os[:].rearrange("p h d -> p (h d)"), gz[:], ALU.mult), reads=['HM', kgz], writes=['YBo'])
            if p == 0:
                if np_ == 128:
                    op('act', f_dma(k.Y[0:64, 1024:2048], YBo[64:128, :]), reads=['YBo'], writes=[('Yb', p)], dma=True)
            else:
                y0 = 128 * p - 64
                op('act', f_dma(k.Y[y0:y0 + np_, 1024:2048], YBo[0:np_, :]), reads=['YBo'], writes=[('Yb', p)], dma=True)
        if hasattr(k, 'YG'):
            pl = NTL - 1
            emit_exchange(k, S, (pl - 2) // 2, [(nm, q) for q in (pl - 2, pl - 1, pl) for nm in ('Ya', 'Yb')])
        S.barrier()
        S.emit(semstack, st)


def phase3(k, semstack):
    nc = k.nc
    SR = 64 * k.NCH
    TQ, TT, NBLK = k.TQ, k.TT, k.NBLK
    if P3S <= 1:
        return
    S = Sched(nc)
    op = S.op
    with ExitStack() as st:
        sb = lambda n, sh, dt=F32: st.enter_context(nc.sbuf_tensor(_un(n), sh, dt))
        ps = lambda n, sh, dt=F32: st.enter_context(nc.psum_tensor(_un(n), sh, dt))
        make_consts(k, S, st)
        gbc = sb("gbc3", [128, D])
        op('sp', f_dma(gbc[:], k.norm_in_g.partition_broadcast(128)), writes=['gbc'], dma=True)
        xt = [sb(f"x3{i}", [128, D]) for i in range(2)]
        hn = [sb(f"hn3{i}", [128, D], BF16) for i in range(2)]
        ss = [sb(f"ss3{i}", [128, 1]) for i in range(2)]
        rs = [sb(f"rs3{i}", [128, 1]) for i in range(2)]
        hTt = [sb(f"hTt{i}", [128, 32, 128], BF16) for i in range(2)]
        yTt = [sb(f"yTt{i}", [128, 64, 128], BF16) for i in range(2)]
        Yl = [sb(f"Yl{i}", [128, 2048], BF16) for i in range(2)]
        yidx = sb("yidx", [128, (TQ // 128) * 4], mybir.dt.int32)
        op('sp', f_dma(yidx[:], k.yrows), writes=['yidx'], dma=True)
        ptr = [ps(f"ptr3{i}", [128, 8, 128], BF16) for i in range(2)]
        tri = 0; si = 0; yi = 0
        for ti in range(TQ // 128):
            i = ti % 2
            tok = ti * 128
            blk, a = tok // TT, tok % TT
            op('sp', f_dma(xt[i][:], k.xq[tok:tok + 128, :]), writes=[('xt', i)], dma=True)
            op('act', f_act(hn[i][:], xt[i][:], AF.Square, accum_out=ss[i][:]), reads=[('xt', i)], writes=[('hn', i), ('ss', i)])
            op('act', f_act(rs[i][:], ss[i][:], AF.Sqrt, bias=EPS, scale=1.0 / D), reads=[('ss', i)], writes=[('rs', i)])
            op('dve', f_recip(rs[i][:], rs[i][:]), reads=[('rs', i)], writes=[('rs', i)])
            op('dve', f_stt(hn[i][:], xt[i][:], rs[i][:, 0:1], gbc[:], ALU.mult, ALU.mult), reads=[('xt', i), ('rs', i), 'gbc'], writes=[('hn', i)])
            for kc0 in range(0, 32, 8):
                j = tri % 2; tri += 1
                for q in range(8):
                    kc = kc0 + q
                    op('pe', f_tr(ptr[j][:, q, :], hn[i][:, kc * 128:(kc + 1) * 128], k.identb[:]), reads=[('hn', i), 'identb'], writes=[('ptr', j)])
                op('act', f_acopy(hTt[i][:, kc0:kc0 + 8, :], ptr[j][:]), reads=[('ptr', j)], writes=[('hTt', i)])
            op('pool', f_dma(k.hTd[blk, :, :, a:a + 128], hTt[i][:]), reads=[('hTt', i)], writes=[('hTd', ti)], dma=True)
            for r in range(4):
                y = yi % 2; yi += 1
                op('pool', (lambda y_, c_: lambda e: e.indirect_dma_start(
                    out=Yl[y_][:, :], out_offset=None, in_=k.YG[:, :],
                    in_offset=bass.IndirectOffsetOnAxis(ap=yidx[:, c_:c_ + 1], axis=0)))(y, ti * 4 + r),
                   reads=['yidx'], writes=[('Yl', y)], dma=True)
                for ab in range(2):
                    j = tri % 2; tri += 1
                    for q in range(8):
                        col = ab * 1024 + q * 128
                        op('pe', f_tr(ptr[j][:, q, :], Yl[y][:, col:col + 128], k.identb[:]), reads=[('Yl', y), 'identb'], writes=[('ptr', j)])
                    dst0 = ab * 32 + r * 8
                    eng = 'dve' if tri % 2 else 'act'
                    fn = f_copy if eng == 'dve' else f_acopy
                    op(eng, fn(yTt[i][:, dst0:dst0 + 8, :], ptr[j][:]), reads=[('ptr', j)], writes=[('yTt', i)])
            op('pool', f_dma(k.yTd[blk, :, :, a:a + 128], yTt[i][:]), reads=[('yTt', i)], writes=[('yTd', ti)], dma=True)
        S.barrier()
        S.emit(semstack, st)
    if P3S <= 2:
        return
    S = Sched(nc)
    op = S.op
    with ExitStack() as st:
        sb = lambda n, sh, dt=F32: st.enter_context(nc.sbuf_tensor(_un(n), sh, dt))
        ps = lambda n, sh, dt=F32: st.enter_context(nc.psum_tensor(_un(n), sh, dt))
        hT = sb("hT3", [128, 32, TT], BF16)
        yT = sb("yT3", [128, 64, TT], BF16)
        mT = sb("mT3", [128, 32, TT], BF16)
        wbuf = [sb(f"wbuf{i}", [128, 4, 32, 128], BF16) for i in range(2)]
        sga = sb("sga", [128, TT]); sgb = sb("sgb", [128, TT]); tmp = sb("tmp3", [128, TT])
        ssf = sb("ssf", [128, 4])
        pq = [ps(f"pq{i}", [128, 512]) for i in range(8)]
        NTT = TT // 128
        gfb = yT[:].rearrange("p a b -> p (a b)").bitcast(F32)
        if TT == 512:
            xrow = hT[:].rearrange("p a b -> p (a b)").bitcast(F32)
            orow = [xrow[:, 0:D], xrow[:, D:2 * D], gfb[:, D:2 * D], gfb[:, 2 * D:3 * D]]
            obase = ['hT', 'hT', 'yT', 'yT']
        else:
            orow_sb = sb("orow", [128, NTT, D])
            orow = [orow_sb[:, tt, :] for tt in range(NTT)]
            obase = [('orowb', tt) for tt in range(NTT)]
        mTf = mT[:].rearrange("p a b -> p (a b)")[:, 0:D]
        wi = 0; pi = 0
        for blk in range(NBLK):
            op('sp', f_dma(hT[:], k.hTd[blk]), writes=['hT'], dma=True)
            op('sp', f_dma(yT[:], k.yTd[blk]), writes=['yT'], dma=True)
            for mc in range(32):
                w = wi % 2; wi += 1
                for q, (src, key, idx) in enumerate(((k.wgb, 'wgb', mc), (k.wgb, 'wgb', 32 + mc), (k.wab, 'wab', mc), (k.wbb, 'wbb', mc))):
                    op('sp', f_dma(wbuf[w][:, q, :, :], src[idx]), writes=[('wbuf', w, q)], dma=True)
                pp = [pq[(pi + q) % 8] for q in range(4)]
                kk_ = [('pq', (pi + q) % 8) for q in range(4)]
                pi += 4
                for q in range(4):
                    for kc in range(32):
                        rhs = hT[:, kc, :] if q < 2 else yT[:, (q - 2) * 32 + kc, :]
                        op('pe', f_mm(pp[q][:, 0:TT], wbuf[w][:, q, kc, :], rhs, kc == 0, kc == 31),
                           reads=[('wbuf', w, q), 'hT' if q < 2 else 'yT'], writes=[kk_[q]])
                op('act', f_act(sga[:], pp[0][:, 0:TT], AF.Sigmoid), reads=[kk_[0]], writes=['sga'])
                op('act', f_act(sgb[:], pp[1][:, 0:TT], AF.Sigmoid), reads=[kk_[1]], writes=['sgb'])
                op('dve', f_tt(sga[:], sga[:], pp[2][:, 0:TT], ALU.mult), reads=['sga', kk_[2]], writes=['sga'])
                op('dve', f_tt(sgb[:], sgb[:], pp[3][:, 0:TT], ALU.mult), reads=['sgb', kk_[3]], writes=['sgb'])
                op('pool', f_tt(mT[:, mc, :], sga[:], sgb[:], ALU.add), reads=['sga', 'sgb'], writes=['mT'])
            op('sp', f_dma(gfb[:, 0:D], k.norm_f_g.partition_broadcast(128)), writes=['yT'], dma=True)
            for tt in range(NTT):
                tok = blk * TT + tt * 128
                op('sp', f_dma(orow[tt], k.xq[tok:tok + 128, :]), writes=[obase[tt], ('orow', tt)], dma=True)
            for cg in range(8):
                w = wi % 2; wi += 1
                wo_t = wbuf[w][:].rearrange("p a b c -> p (a b c)").rearrange("p (m n) -> p m n", n=512)
                op('sp', f_dma(wo_t, k.wob[cg]), writes=[('wbuf', w, q) for q in range(4)], dma=True)
                for tt in range(NTT):
                    b = pi % 8; pi += 1
                    for mc in range(32):
                        op('pe', f_mm(pq[b][:], mT[:, mc, tt * 128:(tt + 1) * 128], wo_t[:, mc, :], mc == 0, mc == 31),
                           reads=['mT'] + [('wbuf', w, q) for q in range(4)], writes=[('pq', b)])
                    op('dve', f_tt(orow[tt][:, cg * 512:(cg + 1) * 512], orow[tt][:, cg * 512:(cg + 1) * 512], pq[b][:], ALU.add),
                       reads=[('orow', tt), ('pq', b), obase[tt]], writes=[('orow', tt)])
            for tt in range(NTT):
                tok = blk * TT + tt * 128
                op('act', f_act(mTf, orow[tt], AF.Square, accum_out=ssf[:, tt:tt + 1]), reads=[('orow', tt)], writes=['mT', ('ssf', tt)])
                op('act', f_act(ssf[:, tt:tt + 1], ssf[:, tt:tt + 1], AF.Sqrt, bias=EPS, scale=1.0 / D), reads=[('ssf', tt)], writes=[('ssf', tt)])
                op('dve', f_recip(ssf[:, tt:tt + 1], ssf[:, tt:tt + 1]), reads=[('ssf', tt)], writes=[('ssf', tt)])
                op('dve', f_stt(orow[tt], orow[tt], ssf[:, tt:tt + 1], gfb[:, 0:D], ALU.mult, ALU.mult),
                   reads=[('orow', tt), ('ssf', tt), 'yT'], writes=[('orow', tt)])
                op('act', f_dma(k.out[tok:tok + 128, :], orow[tt]), reads=[('orow', tt), obase[tt]], writes=[('out', tok)], dma=True)
        S.barrier()
        S.emit(semstack, st)
```
